# Optimizing a Trainium2 kernel written in Bass

```python
import jax
import jax.numpy as jnp
from jax import lax
import numpy as np

D_MODEL = 1024
BATCH = 4
SEQ = 4096
DEPTH = 2
DEC_BATCH = 128
DEC_SEQ = 4
PAST_LEN = 8192
PAGE_SIZE = 128

N_META = 16
HEAD_DIM = 64
N_Q_HEADS = 8
N_KV_HEADS = 2
GROUP = N_Q_HEADS // N_KV_HEADS
WINDOW = 128
BLOCK = 128
ROPE_THETA = 10000.0
CONV_DIM = 512
CONV_W = 3
POOL_WINDOWS = (2, 4, 8, 16)
POOL_GROUPS = len(POOL_WINDOWS)
POOL_GC = D_MODEL // POOL_GROUPS
POOL_HIST = max(POOL_WINDOWS) - 1
D_FF = 2816
RMS_EPS = 1e-6
N_EVEN = (DEPTH + 1) // 2
N_ODD = DEPTH // 2
ATT_W = N_Q_HEADS * HEAD_DIM
KV_W = N_KV_HEADS * HEAD_DIM
IN_W = ATT_W + 2 * KV_W + 3 * CONV_DIM
MIX_W = ATT_W + CONV_DIM
SPLITS = (ATT_W, ATT_W + KV_W, ATT_W + 2 * KV_W, ATT_W + 2 * KV_W + CONV_DIM, ATT_W + 2 * KV_W + 2 * CONV_DIM)

kernel_name = "hybrid_swa_sink_shortconv_pool_macaron_step"


def rmsnorm(x, g):
    xf = x.astype(jnp.float32)
    y = xf * lax.rsqrt(jnp.mean(xf * xf, axis=-1, keepdims=True) + RMS_EPS)
    return (y * g.astype(jnp.float32)).astype(x.dtype)


def swiglu(x, w_gate, w_up, w_down):
    return (jax.nn.silu(x @ w_gate) * (x @ w_up)) @ w_down


def rope(x, pos):
    half = HEAD_DIM // 2
    inv = ROPE_THETA ** (-jnp.arange(half, dtype=jnp.float32) / half)
    ang = pos.astype(jnp.float32)[:, None] * inv[None, :]
    cos = jnp.cos(ang)[None, :, None, :]
    sin = jnp.sin(ang)[None, :, None, :]
    xf = x.astype(jnp.float32)
    x1, x2 = xf[..., :half], xf[..., half:]
    return jnp.concatenate([x1 * cos - x2 * sin, x2 * cos + x1 * sin], axis=-1).astype(x.dtype)


def sink_softmax(s, mask, sink):
    s = jnp.where(mask, s, -jnp.inf)
    sk = sink.astype(jnp.float32)[..., None, None]
    m = jnp.maximum(jnp.max(s, axis=-1, keepdims=True), sk)
    p = jnp.exp(s - m)
    return p / (jnp.sum(p, axis=-1, keepdims=True) + jnp.exp(sk - m))


def swa_prompt(q, k, v, sink):
    b, t = q.shape[:2]
    pad = (-t) % BLOCK
    nb = (t + pad) // BLOCK
    front = lambda a: jnp.pad(a, ((0, 0), (pad, 0)) + ((0, 0),) * (a.ndim - 2))
    qb = front(q).reshape(b, nb, BLOCK, N_KV_HEADS, GROUP, HEAD_DIM)
    kb = front(k).reshape(b, nb, BLOCK, N_KV_HEADS, HEAD_DIM)
    vb = front(v).reshape(b, nb, BLOCK, N_KV_HEADS, HEAD_DIM)
    band = lambda a: jnp.concatenate([jnp.pad(a, ((0, 0), (1, 0), (0, 0), (0, 0), (0, 0)))[:, :-1], a], axis=2)
    kband, vband = band(kb), band(vb)
    s = jnp.einsum("bnqhgd,bnkhd->bnhgqk", qb, kband, preferred_element_type=jnp.float32) * (HEAD_DIM ** -0.5)
    start = jnp.arange(nb, dtype=jnp.int32)[:, None] * BLOCK - pad
    qpos = start + jnp.arange(BLOCK, dtype=jnp.int32)[None, :]
    kpos = start - BLOCK + jnp.arange(2 * BLOCK, dtype=jnp.int32)[None, :]
    dist = qpos[:, :, None] - kpos[:, None, :]
    mask = (kpos[:, None, :] >= 0) & (dist >= 0) & (dist <= WINDOW)
    p = sink_softmax(s, mask[None, :, None, None], sink)
    o = jnp.einsum("bnhgqk,bnkhd->bnqhgd", p.astype(v.dtype), vband)
    return o.reshape(b, nb * BLOCK, ATT_W)[:, pad:]


def swa_sample(q, k, v, k_buf, v_buf, sink, pos):
    bd, s_len = q.shape[:2]
    w = k_buf.shape[1]
    kk = jnp.concatenate([k_buf, k], axis=1)
    vv = jnp.concatenate([v_buf, v], axis=1)
    qg = q.reshape(bd, s_len, N_KV_HEADS, GROUP, HEAD_DIM)
    s = jnp.einsum("bqhgd,bkhd->bhgqk", qg, kk, preferred_element_type=jnp.float32) * (HEAD_DIM ** -0.5)
    kpos = pos[0] - w + jnp.arange(w + s_len, dtype=jnp.int32)
    dist = pos[:, None] - kpos[None, :]
    mask = (dist >= 0) & (dist <= WINDOW)
    p = sink_softmax(s, mask, sink)
    o = jnp.einsum("bhgqk,bkhd->bqhgd", p.astype(vv.dtype), vv).reshape(bd, s_len, ATT_W)
    return o, kk[:, -w:], vv[:, -w:]


def causal_dwconv(u_ext, w, n_out):
    out = w[0] * u_ext[:, 0:n_out]
    for j in range(1, CONV_W):
        out = out + w[j] * u_ext[:, j:j + n_out]
    return out


def pool_mixer(u_ext, pos_ext, n_out, w_group, scale):
    b, l = u_ext.shape[:2]
    uf = u_ext.astype(jnp.float32)
    csum = jnp.pad(jnp.cumsum(uf, axis=1), ((0, 0), (1, 0), (0, 0)))
    outs = []
    for gi, win in enumerate(POOL_WINDOWS):
        cg = csum[..., gi * POOL_GC:(gi + 1) * POOL_GC]
        shifted = jnp.pad(cg, ((0, 0), (win, 0), (0, 0)))[:, :l + 1]
        wsum = cg[:, 1:] - shifted[:, 1:]
        cnt = jnp.minimum(win, pos_ext + 1).astype(jnp.float32)
        outs.append(wsum / cnt[None, :, None] - uf[..., gi * POOL_GC:(gi + 1) * POOL_GC])
    p = jnp.stack(outs, axis=2)[:, -n_out:].astype(u_ext.dtype)
    z = jnp.einsum("blgc,gce->blge", p, w_group).reshape(b, n_out, D_MODEL)
    return z * scale


def run_trunk(x, pos, past, ln_gain, ffn_w_gate, ffn_w_up, ffn_w_down, mix_w_in, attn_sink, conv_w, mix_w_out, pool_w, pool_scale, final_gain):
    b, l = x.shape[:2]
    k_rows, v_rows, conv_rows, pool_rows = [], [], [], []
    for layer in range(DEPTH):
        g = ln_gain[layer]
        x = x + 0.5 * swiglu(rmsnorm(x, g[0]), ffn_w_gate[layer, 0], ffn_w_up[layer, 0], ffn_w_down[layer, 0])
        h = rmsnorm(x, g[1])
        if layer % 2 == 0:
            i = layer // 2
            q, k, v, gate_b, gate_c, hc = jnp.split(h @ mix_w_in[i], SPLITS, axis=-1)
            q = rope(q.reshape(b, l, N_Q_HEADS, HEAD_DIM), pos)
            k = rope(k.reshape(b, l, N_KV_HEADS, HEAD_DIM), pos)
            v = v.reshape(b, l, N_KV_HEADS, HEAD_DIM)
            if past is None:
                att = swa_prompt(q, k, v, attn_sink[i])
                keep = min(WINDOW, PAST_LEN)
                k_keep, v_keep = k[:, -keep:], v[:, -keep:]
                conv_hist = jnp.zeros((b, CONV_W - 1, CONV_DIM), h.dtype)
            else:
                att, k_keep, v_keep = swa_sample(q, k, v, past[0][i], past[1][i], attn_sink[i], pos)
                conv_hist = past[2][i]
            u_ext = jnp.concatenate([conv_hist.astype(h.dtype), gate_c * hc], axis=1)
            conv = causal_dwconv(u_ext, conv_w[i], l)
            mix = jnp.concatenate([att, gate_b * conv], axis=-1) @ mix_w_out[i]
            k_rows.append(k_keep)
            v_rows.append(v_keep)
            conv_rows.append(u_ext[:, -(CONV_W - 1):])
        else:
            j = layer // 2
            if past is None:
                u_ext, pos_ext = h, pos
            else:
                u_ext = jnp.concatenate([past[3][j].astype(h.dtype), h], axis=1)
                pos_ext = jnp.concatenate([pos[0] - POOL_HIST + jnp.arange(POOL_HIST, dtype=jnp.int32), pos])
            mix = pool_mixer(u_ext, pos_ext, l, pool_w[j], pool_scale[j])
            pool_rows.append(u_ext[:, -POOL_HIST:])
        x = x + mix
        x = x + 0.5 * swiglu(rmsnorm(x, g[2]), ffn_w_gate[layer, 1], ffn_w_up[layer, 1], ffn_w_down[layer, 1])
    return rmsnorm(x, final_gain), jnp.stack(k_rows), jnp.stack(v_rows), jnp.stack(conv_rows), jnp.stack(pool_rows)


def setup_inputs(seed: int = 0) -> dict:
    key = jax.random.key(seed)
    ks = jax.random.split(key, 18)
    nrm = lambda k, shape, scale: scale * jax.random.normal(k, shape, jnp.float32)
    winb = min(WINDOW, PAST_LEN)
    return {
        "x_prompt": nrm(ks[0], (BATCH, SEQ, D_MODEL), 1.0),
        "x_sample": nrm(ks[1], (DEC_BATCH, DEC_SEQ, D_MODEL), 1.0),
        "cache_k": nrm(ks[2], (N_EVEN, DEC_BATCH, winb, N_KV_HEADS, HEAD_DIM), 1.0),
        "cache_v": nrm(ks[3], (N_EVEN, DEC_BATCH, winb, N_KV_HEADS, HEAD_DIM), 1.0),
        "state_conv": nrm(ks[4], (N_EVEN, DEC_BATCH, CONV_W - 1, CONV_DIM), 1.0),
        "state_pool": nrm(ks[5], (N_ODD, DEC_BATCH, POOL_HIST, D_MODEL), 1.0),
        "meta_tokens": nrm(ks[6], (N_META, D_MODEL), 1.0),
        "ln_gain": 1.0 + nrm(ks[7], (DEPTH, 3, D_MODEL), 0.02),
        "ffn_w_gate": nrm(ks[8], (DEPTH, 2, D_MODEL, D_FF), D_MODEL ** -0.5),
        "ffn_w_up": nrm(ks[9], (DEPTH, 2, D_MODEL, D_FF), D_MODEL ** -0.5),
        "ffn_w_down": nrm(ks[10], (DEPTH, 2, D_FF, D_MODEL), D_FF ** -0.5),
        "mix_w_in": nrm(ks[11], (N_EVEN, D_MODEL, IN_W), D_MODEL ** -0.5),
        "attn_sink": nrm(ks[12], (N_EVEN, N_KV_HEADS, GROUP), 1.0),
        "conv_w": nrm(ks[13], (N_EVEN, CONV_W, CONV_DIM), CONV_W ** -0.5),
        "mix_w_out": nrm(ks[14], (N_EVEN, MIX_W, D_MODEL), MIX_W ** -0.5),
        "pool_w": nrm(ks[15], (N_ODD, POOL_GROUPS, POOL_GC, POOL_GC), POOL_GC ** -0.5),
        "pool_scale": 1.0 + nrm(ks[16], (N_ODD, D_MODEL), 0.02),
        "final_gain": 1.0 + nrm(ks[17], (D_MODEL,), 0.02),
    }


def reference(x_prompt, x_sample, cache_k, cache_v, state_conv, state_pool, meta_tokens, ln_gain, ffn_w_gate, ffn_w_up, ffn_w_down, mix_w_in, attn_sink, conv_w, mix_w_out, pool_w, pool_scale, final_gain):
    weights = (ln_gain, ffn_w_gate, ffn_w_up, ffn_w_down, mix_w_in, attn_sink, conv_w, mix_w_out, pool_w, pool_scale, final_gain)
    b = x_prompt.shape[0]
    meta = jnp.broadcast_to(meta_tokens.astype(x_prompt.dtype)[None], (b, N_META, D_MODEL))
    xp = jnp.concatenate([meta, x_prompt], axis=1)
    pos_p = jnp.arange(xp.shape[1], dtype=jnp.int32)
    y_full, k_p, v_p, conv_p, pool_p = run_trunk(xp, pos_p, None, *weights)
    y_prompt = y_full[:, N_META:]
    pos_s = PAST_LEN + jnp.arange(x_sample.shape[1], dtype=jnp.int32)
    y_sample, k_s, v_s, conv_s, pool_s = run_trunk(x_sample, pos_s, (cache_k, cache_v, state_conv, state_pool), *weights)
    return (y_prompt, y_sample, k_p, v_p, conv_p, pool_p, k_s, v_s, conv_s, pool_s)
```

```python
import numpy as np
import concourse.bass as bass
import concourse.mybir as mybir
from concourse.bass_utils import run_bass_kernel_spmd
from contextlib import ExitStack

F32 = mybir.dt.float32
BF16 = mybir.dt.bfloat16
AF = mybir.ActivationFunctionType
ALU = mybir.AluOpType
AX = mybir.AxisListType

D = 1024
KC = 8
NJ = 22
TCORE = 2368
TG = 1024
NSLOT = 6
EPS = 1e-6
NEG = -30000.0
C_GAIN, C_CONVW, C_PSC, C_SINK, C_SINKS, C_MASK, C_MASKS, C_ID = 0, 56, 68, 76, 84, 86, 854, 1014
NCST = 1142


class _Stop(Exception):
    pass


class Grp:
    def __init__(self, lo, n, A, B, M, S):
        self.lo, self.n, self.A, self.B, self.M, self.S = lo, n, A, B, M, S
        self.noA = (B[1] - 16) if B else 0
        self.out = M[0]
        self.pend = M[1]


GROUPS = [
    Grp(0, 1024, (0, 128), (128, 256), (256, 1024), None),
    Grp(1024, 1024, None, None, (0, 1024), None),
    Grp(2048, 320, None, None, (0, 256), (256, 320)),
]


def subtiles_flat(lo, hi):
    out = []
    c = lo
    while c < hi:
        n = min(512, hi - c)
        out.append((c, c + n, 'p'))
        c += n
    return out


def subtiles(g, lo, hi):
    out = []
    pe = min(hi, g.pend)
    c = lo
    while c < pe:
        n = min(512, pe - c)
        out.append((c, c + n, 'p'))
        c += n
    if g.S and hi > g.S[0]:
        out.append((max(lo, g.S[0]), hi, 's'))
    return out


PSUM_NAMES = {"gu0", "gu1", "gu2", "gu3", "yb0", "yb1", "tp", "tpb"}


class Acc:
    __slots__ = ("ap", "r")

    def __init__(self, ap, r):
        self.ap, self.r = ap, r


class LT:
    def __init__(self, phys, ap, base, esize, fshape):
        self.phys, self.ap, self.base, self.esize, self.fshape = phys, ap, base, esize, tuple(fshape)
        st = [1] * len(fshape)
        for i in range(len(fshape) - 2, -1, -1):
            st[i] = st[i + 1] * fshape[i + 1]
        self.st = st

    def __call__(self, *idx, p=None):
        idx = list(idx) + [slice(None)] * (len(self.fshape) - len(idx))
        key = (slice(p[0], p[1]) if p else slice(None),) + tuple(idx)
        ap = self.ap[key]
        offs = [0]
        for d in range(len(idx) - 1):
            ix = idx[d]
            if isinstance(ix, int):
                offs = [o + ix * self.st[d] for o in offs]
            else:
                lo, hi, step = ix.indices(self.fshape[d])
                offs = [o + i * self.st[d] for o in offs for i in range(lo, hi, step)]
        ix = idx[-1]
        if isinstance(ix, int):
            lo, hi = ix, ix + 1
        else:
            lo, hi, step = ix.indices(self.fshape[-1])
        rs = [(self.phys, self.base + (o + lo) * self.esize, self.base + (o + hi) * self.esize) for o in offs]
        return Acc(ap, rs)

    def whole(self):
        n = 1
        for s in self.fshape:
            n *= s
        return [(self.phys, self.base, self.base + n * self.esize)]


class Sched:
    def __init__(self, dry):
        self.dry = dry
        self.ops = []
        self.recs = {}
        self.sem_total = {}
        self.final_deps = []
        self.bank_last = {}

    def op(self, eng, fn, reads=(), writes=(), dma=None, is_out=False):
        if self.dry:
            return None
        oid = len(self.ops)
        deps = set()
        for (ph, lo, hi) in reads:
            for r in self.recs.get(ph, ()):
                if r[3] and r[0] < hi and lo < r[1]:
                    deps.add(r[2])
        for (ph, lo, hi) in writes:
            for r in self.recs.get(ph, ()):
                if r[0] < hi and lo < r[1]:
                    deps.add(r[2])
        banks = {ph for (ph, lo, hi) in reads if ph in PSUM_NAMES} | {ph for (ph, lo, hi) in writes if ph in PSUM_NAMES}
        for bk in banks:
            la = self.bank_last.setdefault(bk, {})
            for e2, o2 in la.items():
                if e2 != eng:
                    deps.add(o2)
            la[eng] = oid
        for (ph, lo, hi) in writes:
            lst = self.recs.setdefault(ph, [])
            lst[:] = [r for r in lst if not (lo <= r[0] and r[1] <= hi)]
            lst.append((lo, hi, oid, True, eng))
        for (ph, lo, hi) in reads:
            lst = self.recs.setdefault(ph, [])
            if dma is None:
                lst[:] = [r for r in lst if not ((not r[3]) and r[4] == eng and r[0] == lo and r[1] == hi)]
            lst.append((lo, hi, oid, False, eng))
        waits = {}
        for d in deps:
            od = self.ops[d]
            if od["dma"] is not None:
                k = ("dma", od["dma"])
                waits[k] = self.sem_total[od["dma"]]
            else:
                if od["eng"] == "pe" and eng == "pe":
                    continue
                od["signal"] = True
                k = ("eng", od["eng"])
                waits[k] = max(waits.get(k, -1), d)
        if dma is not None:
            self.sem_total[dma] = self.sem_total.get(dma, 0) + 16
        self.ops.append(dict(eng=eng, fn=fn, waits=waits, dma=dma, signal=False))
        if is_out:
            self.final_deps.append(oid)
        return oid

    def finalize(self):
        waits = {}
        for d in self.final_deps:
            od = self.ops[d]
            waits[("dma", od["dma"])] = self.sem_total[od["dma"]]
        self.ops.append(dict(eng="sp", fn=None, waits=waits, dma=None, signal=False))


def build_program(dbg_point=None):
    nc = bass.Bass("TRN2", target_bir_lowering=False)

    def din(name, shape):
        return nc.dram_tensor(name, list(shape), F32, kind="ExternalInput").ap()

    def dout(name, shape):
        return nc.dram_tensor(name, list(shape), F32, kind="ExternalOutput").ap()

    xin = din("xin", [TCORE, D])
    wgu = din("wgu", [4, NJ, 128, 2048])
    wd = din("wd", [4, 8, 2, 128, 1408])
    wmi = din("wmi", [23, 128, 1024])
    wmo = din("wmo", [8, 128, 1024])
    wpl = din("wpl", [128, 2048])
    cosd = din("cosd", [128, TCORE])
    sind = din("sind", [128, TCORE])
    cstd = din("cst", [128, NCST])
    ckd = din("ck", [16, 128, 128])
    cvd = din("cv", [16, 128, 128])
    sconvd = din("sconv", [16, 2, 512])
    spoold = din("spool", [16, 15, 1024])
    y_main = dout("y_main", [2048, D])
    y_samp = dout("y_samp", [64, D])
    kpd = dout("kp", [128, 128])
    vpd = dout("vp", [128, 128])
    convpd = dout("convp", [2, 512])
    poolpd = dout("poolp", [15, 1024])
    ksd = dout("ks", [16, 128, 128])
    vsd = dout("vs", [16, 128, 128])
    convsd = dout("convs", [16, 2, 512])
    poolsd = dout("pools", [16, 15, 1024])
    dbgd = dout("dbg", [128, 8 * TG]) if dbg_point is not None else None

    es = ExitStack()
    with es:
        def sb(name, shape, dt):
            return es.enter_context(nc.sbuf_tensor("sb_" + name, list(shape), dt))

        def ps(name, shape, dt):
            return es.enter_context(nc.psum_tensor("ps_" + name, list(shape), dt))

        def mk(t, name, fshape, dt):
            esz = 4 if dt == F32 else 2
            return LT(name, t[:] if hasattr(t, "__getitem__") else t, 0, esz, fshape)

        xT_t = sb("xT", [128, KC, TG], F32); xT = mk(xT_t, "xT", (KC, TG), F32)
        hT_t = sb("hT", [128, KC, TG], BF16); hT = mk(hT_t, "hT", (KC, TG), BF16)
        NOVL = 29696
        ovl_t = sb("ovl", [128, NOVL], BF16)

        def ov(byte_off, fshape, dt):
            esz = 4 if dt == F32 else 2
            n = 1
            for s in fshape:
                n *= s
            assert byte_off % 4 == 0 and byte_off + n * esz <= NOVL * 2, (byte_off, fshape)
            ap = ovl_t[:, byte_off // 2: byte_off // 2 + n * esz // 2]
            if dt == F32:
                ap = ap.bitcast(F32)
            if len(fshape) == 2:
                ap = ap.rearrange("p (a b) -> p a b", a=fshape[0])
            elif len(fshape) == 3:
                ap = ap.rearrange("p (a b c) -> p a b c", a=fshape[0], b=fshape[1])
            return LT("ovl", ap, byte_off, esz, fshape)

        lstg = ov(0, (8, 1024), F32)
        aT = ov(0, (NJ, TG), BF16)
        uT = ov(0, (4, TG + 2), F32)
        qT = ov(16416, (4, TG), BF16)
        attT = ov(24608, (4, TG), BF16)
        gcv = ov(32800, (4, TG), BF16)
        qpad = ov(40992, (16, 128), BF16)
        kctok = ov(45088, (16, 128), BF16)
        Vs = ov(49184, (16, 128), BF16)
        kTs = ov(53280, (16, 128), BF16)
        h1bf = ov(0, (8, 528), BF16)
        ppT = ov(8448, (8, 512), BF16)
        hsx = ov(45088, (8, 304), F32)
        pa = [ov(26368 + 1216 * i, (304,), F32) for i in range(4)]
        Dg = ov(31232, (4, 2, 128), BF16)
        wr_t = sb("wring", [128, NSLOT, 2048], BF16); wring = mk(wr_t, "wring", (NSLOT, 2048), BF16)
        kT_t = sb("kT", [128, 128 + TG], BF16); kT = mk(kT_t, "kT", (128 + TG,), BF16)
        vT_t = sb("vT", [128, TG], BF16); vT = mk(vT_t, "vT", (TG,), BF16)
        Vtok_t = sb("Vtok", [128, 9, 128], BF16); Vtok = mk(Vtok_t, "Vtok", (9, 128), BF16)
        kv32_t = sb("kv32", [128, 2, 192], F32); kv32 = mk(kv32_t, "kv32", (2, 192), F32)
        cos_t = sb("cos", [128, TG], F32); cosT = mk(cos_t, "cos", (TG,), F32)
        sin_t = sb("sin", [128, TG], F32); sinT = mk(sin_t, "sin", (TG,), F32)
        rstd_t = sb("rstd", [128, TG], F32); rstd = mk(rstd_t, "rstd", (TG,), F32)
        rtT_t = sb("rtT", [128, 8], F32); rtT = mk(rtT_t, "rtT", (8,), F32)
        rrT_t = sb("rrT", [128, 8], F32); rrT = mk(rrT_t, "rrT", (8,), F32)
        dmy_t = sb("dmy", [128, 4], F32); dmy = mk(dmy_t, "dmy", (4,), F32)
        ones32_t = sb("ones32", [128, 128], F32); ones32 = mk(ones32_t, "ones32", (128,), F32)
        sq_t = sb("sq", [128, 3, 512], BF16); sq = mk(sq_t, "sq", (3, 512), BF16)
        sg_t = sb("sg", [128, 2, 512], F32); sg = mk(sg_t, "sg", (2, 512), F32)
        rp_t = sb("rp", [128, 4, 512], F32); rp = mk(rp_t, "rp", (4, 512), F32)
        s_t = sb("s_sc", [128, 2, 256], F32); s_sc = mk(s_t, "s_sc", (2, 256), F32)
        pe_t = sb("pexp", [128, 2, 256], F32); pexp = mk(pe_t, "pexp", (2, 256), F32)
        pn_t = sb("pn", [128, 2, 256], BF16); pn = mk(pn_t, "pn", (2, 256), BF16)
        pT_t = sb("pT", [128, 2, 256], BF16); pT = mk(pT_t, "pT", (2, 256), BF16)
        st_t = sb("stat", [128, 2, 8], F32); stat = mk(st_t, "stat", (2, 8), F32)
        stg_t = sb("stg", [128, 2, 1024], F32); stg = mk(stg_t, "stg", (2, 1024), F32)
        cst_t = sb("cst", [128, NCST], F32); cst = mk(cst_t, "cst", (NCST,), F32)
        idb_t = sb("identb", [128, 128], BF16); identb = mk(idb_t, "identb", (128,), BF16)
        one_t = sb("onesD", [128, 128], BF16); onesD = mk(one_t, "onesD", (128,), BF16)
        h1c_t = sb("h1c", [128, 8, 15], F32); h1c = mk(h1c_t, "h1c", (8, 15), F32)
        h1b_t = sb("h1b", [128, 8, 15], F32); h1b = mk(h1b_t, "h1b", (8, 15), F32)
        ucar_t = sb("ucar", [128, 4, 2], F32); ucar = mk(ucar_t, "ucar", (4, 2), F32)
        usx_t = sb("usx", [128, 4, 96], F32); usx = mk(usx_t, "usx", (4, 96), F32)
        knc_t = sb("knc", [128, 2, 32], BF16); knc = mk(knc_t, "knc", (2, 32), BF16)
        vnc_t = sb("vnc", [128, 2, 32], BF16); vnc = mk(vnc_t, "vnc", (2, 32), BF16)
        vno_t = sb("vno", [128, 2, 128], BF16); vno = mk(vno_t, "vno", (2, 128), BF16)
        gu = []
        gu2b = []
        for i in range(2):
            t = ps(f"gupair{i}", [128, 1024], F32)
            gu2b.append(t)
            gu.append(LT(f"gu{2 * i}", t[:, 0:512], 0, 4, (512,)))
            gu.append(LT(f"gu{2 * i + 1}", t[:, 512:1024], 0, 4, (512,)))
        yb = []
        for i in range(2):
            t = ps(f"yb{i}", [128, 512], F32)
            yb.append(mk(t, f"yb{i}", (512,), F32))
        ptb = [LT(f"gu{i}", gu[i].ap.bitcast(BF16), 0, 2, (1024,)) for i in (2, 3)]
        tp_t = ps("tp", [128, 512], F32); tp = mk(tp_t, "tp", (512,), F32)
        tpb_t = ps("tpb", [128, 1024], BF16); tpb = mk(tpb_t, "tpb", (1024,), BF16)
        ptx = [LT("tpb", tpb_t[:].rearrange("p (a b) -> p a b", a=8), 0, 2, (8, 128)),
               LT("tp", tp_t[:].bitcast(BF16).rearrange("p (a b) -> p a b", a=8), 0, 2, (8, 128))]
        pb4_t = sb("pb4", [128, 2, 4, 256], F32); pb4 = mk(pb4_t, "pb4", (2, 4, 256), F32)
        pn4_t = sb("pn4", [128, 2, 4, 256], BF16); pn4 = mk(pn4_t, "pn4", (2, 4, 256), BF16)
        pT4_t = sb("pT4", [128, 2, 8, 128], BF16); pT4 = mk(pT4_t, "pT4", (2, 8, 128), BF16)
        st4_t = sb("st4", [128, 2, 16], F32); st4 = mk(st4_t, "st4", (2, 16), F32)
        maskb_t = sb("maskb", [128, 3, 256], BF16); maskb = mk(maskb_t, "maskb", (3, 256), BF16)
        nsm_t = sb("nsm", [128, 2], F32); nsm = mk(nsm_t, "nsm", (2,), F32)

        def cs(off, n=1):
            return cst(slice(off, off + n))

        ident = LT("cst", cst_t[:, C_ID:C_ID + 128], C_ID * 4, 4, (128,))

        def emit(S, wspecs):
            cnt = dict(gu=0, yb=0, io=0, sq=0, sg=0, att=0, cp=0, w=0, wiss=0, tb=0, fo=0, whold=None, pb=0)

            def OP(eng, f, reads, writes, **kw):
                rr = []
                for a in reads:
                    rr += a.r if isinstance(a, Acc) else a
                ww = []
                for a in writes:
                    ww += a.r if isinstance(a, Acc) else a
                return S.op(eng, f, rr, ww, **kw)

            def wissue(t):
                spec = wspecs[t]
                slot = t % NSLOT
                ncols = spec[1]
                dst = wring(slot, slice(0, ncols))
                src = spec[0]
                OP("pool", lambda e: e.dma_start(out=dst.ap, in_=src), [], [wring(slot)], dma=f"w{slot}")

            def wnext(src_ap, ncols, fshape):
                i = cnt["w"]
                cnt["w"] += 1
                if S.dry:
                    wspecs.append((src_ap, ncols))
                    slot = i % NSLOT
                else:
                    lim = min(i + NSLOT - 1, len(wspecs))
                    if cnt["whold"] is not None:
                        lim = min(lim, cnt["whold"] + NSLOT)
                    while cnt["wiss"] < lim:
                        wissue(cnt["wiss"])
                        cnt["wiss"] += 1
                    slot = i % NSLOT
                ap = wr_t[:, slot, 0:ncols]
                if len(fshape) == 2:
                    ap = ap.rearrange("p (a b) -> p a b", a=fshape[0])
                else:
                    ap = ap.rearrange("p (a b c) -> p a b c", a=fshape[0], b=fshape[1])
                return LT("wring", ap, slot * 4096, 2, fshape)

            def mm(out, lhsT, rhs, start, stop):
                OP("pe", lambda e: e.matmul(out.ap, lhsT=lhsT.ap, rhs=rhs.ap, start=start, stop=stop),
                   [lhsT, rhs], [out])

            def tr(out, in_, idn):
                OP("pe", lambda e: e.transpose(out.ap, in_.ap, idn.ap), [in_, idn], [out])

            def copy(out, in_, eng=None):
                if eng is None:
                    eng = "act" if cnt["cp"] % 2 == 0 else "dve"
                    cnt["cp"] += 1
                if eng == "act":
                    OP("act", lambda e: e.copy(out=out.ap, in_=in_.ap), [in_], [out])
                else:
                    OP("dve", lambda e: e.tensor_copy(out=out.ap, in_=in_.ap), [in_], [out])

            def tt(out, a, b, op, eng="dve"):
                OP(eng, lambda e: e.tensor_tensor(out=out.ap, in0=a.ap, in1=b.ap, op=op), [a, b], [out])

            def stt(out, a, scalar, b, op0, op1):
                sc = scalar.ap if isinstance(scalar, Acc) else scalar
                rd = [a, b] + ([scalar] if isinstance(scalar, Acc) else [])
                OP("dve", lambda e: e.scalar_tensor_tensor(out=out.ap, in0=a.ap, scalar=sc, in1=b.ap, op0=op0, op1=op1),
                   rd, [out])

            def ts(out, a, s1, s2, op0, op1=None):
                s1a = s1.ap if isinstance(s1, Acc) else s1
                s2a = s2.ap if isinstance(s2, Acc) else s2
                rd = [a] + [x for x in (s1, s2) if isinstance(x, Acc)]
                if op1 is None:
                    OP("dve", lambda e: e.tensor_scalar(out=out.ap, in0=a.ap, scalar1=s1a, scalar2=None, op0=op0), rd, [out])
                else:
                    OP("dve", lambda e: e.tensor_scalar(out=out.ap, in0=a.ap, scalar1=s1a, scalar2=s2a, op0=op0, op1=op1), rd, [out])

            def act(out, in_, func, bias=None, scale=None, accum=None):
                kw = {}
                rd = [in_]
                wr = [out]
                if bias is not None:
                    kw["bias"] = bias.ap if isinstance(bias, Acc) else bias
                    if isinstance(bias, Acc):
                        rd.append(bias)
                if scale is not None:
                    kw["scale"] = scale.ap if isinstance(scale, Acc) else scale
                    if isinstance(scale, Acc):
                        rd.append(scale)
                if accum is not None:
                    kw["accum_out"] = accum.ap
                    wr.append(accum)
                OP("act", lambda e: e.activation(out=out.ap, in_=in_.ap, func=func, **kw), rd, wr)

            def dma(eng, out_ap, in_ap, reads, writes, key, is_out=False):
                OP(eng, lambda e: e.dma_start(out=out_ap, in_=in_ap), reads, writes, dma=key, is_out=is_out)

            def gu_pair():
                p = cnt["gu"] % 2
                cnt["gu"] += 1
                return gu[2 * p], gu[2 * p + 1]

            def gu_one():
                p = cnt["gu"] % 2
                cnt["gu"] += 1
                return gu[2 * p]

            def y_bank():
                p = cnt["yb"] % 2
                cnt["yb"] += 1
                return yb[p]

            def io_slot():
                p = cnt["io"] % 2
                cnt["io"] += 1
                return p

            dma("sp", cst_t[:], cstd[:], [], [cst.whole()], "cst")
            OP("dve", lambda e: e.memset(one_t[:], 1.0 / 1024.0), [], [onesD.whole()])
            OP("dve", lambda e: e.memset(ones32_t[:], 1.0), [], [ones32.whole()])
            OP("dve", lambda e: e.memset(dmy_t[:], 1.0), [], [dmy.whole()])
            copy(identb(), ident(), eng="dve")
            src_m = LT("cst", cst_t[:, C_MASK:C_MASK + 768].rearrange("p (a b) -> p a b", a=3), C_MASK * 4, 4, (3, 256))
            copy(maskb(), src_m(), eng="dve")
            for kvh_ in range(2):
                OP("dve", lambda e, k_=kvh_: e.tensor_reduce(out=nsm_t[:, k_:k_ + 1], in_=cst_t[:, C_SINK + 4 * k_:C_SINK + 4 * k_ + 4], axis=AX.X, op=ALU.max),
                   [cst(slice(C_SINK + 4 * kvh_, C_SINK + 4 * kvh_ + 4))], [nsm(slice(kvh_, kvh_ + 1))])
            ts(nsm(), nsm(), -1.0, None, ALU.mult)

            def tbank():
                bl = [tp, yb[0], yb[1], gu[0], gu[1], gu[2], gu[3]]
                b_ = bl[cnt["tb"] % len(bl)]
                cnt["tb"] += 1
                return b_

            def load_x(g):
                for bi_, c0 in enumerate(range(0, g.n, 128)):
                    n = min(128, g.n - c0)
                    sl = bi_ % 8
                    dma("sp", lstg.ap[0:n, sl, :], xin[g.lo + c0: g.lo + c0 + n, :], [], [lstg(sl)], f"ld{sl}")
                    for half in range(2):
                        bk = tbank()
                        for kk in range(4):
                            k = half * 4 + kk
                            tr(bk(slice(kk * 128, kk * 128 + n)), lstg(sl, slice(k * 128, (k + 1) * 128), p=(0, n)),
                               LT("cst", cst_t[0:n, C_ID:C_ID + n], C_ID * 4, 4, (n,))())
                        src = LT(bk.phys, bk.ap.rearrange("p (a b) -> p a b", a=4), 0, 4, (4, 128))
                        copy(xT(slice(half * 4, half * 4 + 4), slice(c0, c0 + n)), src(slice(0, 4), slice(0, n)))

            sq8 = LT("rp", rp_t[:].rearrange("p a b -> p (a b)").bitcast(BF16).rearrange("p (a b) -> p a b", a=8), 0, 2, (8, 512))

            def norm_squares(c0, c1, nb):
                n = c1 - c0
                for k in range(KC):
                    act(sq8(k, slice(0, n)), xT(k, slice(c0, c1)), AF.Square)

            dgv = LT("s_sc", s_t[:].rearrange("p a b -> p (a b)").rearrange("p (a b) -> p a b", a=4), 0, 4, (4, 128))

            def preload(func, slot):
                act(dmy(slice(slot + 1, slot + 2)), dmy(slice(0, 1)), func)

            def norm_stats(c0, c1, nb, col0):
                n = c1 - c0
                nblk = (n + 127) // 128
                for b_ in range(nblk):
                    r0 = b_ * 128
                    nr = min(128, n - r0)
                    for k in range(KC):
                        mm(nb(slice(col0 + b_, col0 + b_ + 1), p=(0, nr)), sq8(k, slice(r0, r0 + nr)), onesD(slice(0, 1)), k == 0, k == KC - 1)
                return [min(128, n - b_ * 128) for b_ in range(nblk)]

            def norm_rsqrt(nb, ncols, rows=None):
                if rows is None:
                    rows = [128] * ncols
                c = 0
                while c < ncols:
                    e_ = c
                    while e_ < ncols and rows[e_] == rows[c]:
                        e_ += 1
                    pr = (0, rows[c])
                    act(rtT(slice(c, e_), p=pr), nb(slice(c, e_), p=pr), AF.Sqrt, bias=EPS, scale=1.0)
                    OP("dve", lambda e, o=rrT(slice(c, e_), p=pr), i=rtT(slice(c, e_), p=pr): e.reciprocal(out=o.ap, in_=i.ap),
                       [rtT(slice(c, e_), p=pr)], [rrT(slice(c, e_), p=pr)])
                    c = e_

            def norm_apply(c0, c1, gi, col0, write_h=True):
                n = c1 - c0
                nblk = (n + 127) // 128
                rb = y_bank()
                for b_ in range(nblk):
                    r0 = b_ * 128
                    nr = min(128, n - r0)
                    idn = LT("cst", cst_t[0:nr, C_ID:C_ID + nr], C_ID * 4, 4, (nr,))()
                    ts(dgv(b_, slice(0, nr), p=(0, nr)), idn, rrT(slice(col0 + b_, col0 + b_ + 1), p=(0, nr)), None, ALU.mult)
                    mm(rb(slice(r0, r0 + nr)), ones32(p=(0, nr)), dgv(b_, slice(0, nr), p=(0, nr)), True, True)
                if write_h:
                    for k in range(KC):
                        stt(hT(k, slice(c0, c1)), xT(k, slice(c0, c1)), cs(C_GAIN + gi * 8 + k),
                            rb(slice(0, n)), ALU.mult, ALU.mult)
                else:
                    copy(rstd(slice(c0, c1)), rb(slice(0, n)), eng="act")

            def norm(g, lo, hi, gi, write_h=True, flat=False, after=None):
                subs_ = subtiles_flat(lo, hi) if flat else subtiles(g, lo, hi)
                nb = y_bank()
                cols = []
                col = 0
                rows = []
                for (c0, c1, typ) in subs_:
                    norm_squares(c0, c1, nb)
                    cols.append(col)
                    rows += norm_stats(c0, c1, nb, col)
                    col = len(rows)
                norm_rsqrt(nb, col, rows)
                if after is not None:
                    preload(after, 1)
                for (c0, c1, typ), cl in zip(subs_, cols):
                    norm_apply(c0, c1, gi, cl, write_h)

            def ffn(g, lo, hi, f, gi):
                subs = subtiles_flat(lo, hi)

                def gu_step(j, w, c0, c1):
                    n = c1 - c0
                    gb, ub = gu_pair()
                    for k in range(KC):
                        mm(gb(slice(0, n)), w(0, k), hT(k, slice(c0, c1)), k == 0, k == KC - 1)
                    for k in range(KC):
                        mm(ub(slice(0, n)), w(1, k), hT(k, slice(c0, c1)), k == 0, k == KC - 1)
                    sl = cnt["sg"] % 2
                    cnt["sg"] += 1
                    act(sg(sl, slice(0, n)), gb(slice(0, n)), AF.Silu)
                    tt(aT(j, slice(c0, c1)), sg(sl, slice(0, n)), ub(slice(0, n)), ALU.mult)

                JH = 4 if len(subs) > 1 else 0
                ws = {}
                nb0 = y_bank()
                cols = []
                col = 0
                rows = []
                for (c0, c1, typ) in subs:
                    norm_squares(c0, c1, nb0)
                    cols.append(col)
                    rows += norm_stats(c0, c1, nb0, col)
                    col = len(rows)
                norm_rsqrt(nb0, col, rows)
                preload(AF.Silu, 1)
                norm_apply(subs[0][0], subs[0][1], gi, cols[0])
                if JH:
                    assert len(subs) == 2
                    cnt["whold"] = cnt["w"]
                    ws[0] = wnext(wgu[f, 0], 2048, (2, KC, 128))
                    gu_step(0, ws[0], subs[0][0], subs[0][1])
                    norm_apply(subs[1][0], subs[1][1], gi, cols[1])
                    for j in range(1, JH):
                        ws[j] = wnext(wgu[f, j], 2048, (2, KC, 128))
                        gu_step(j, ws[j], subs[0][0], subs[0][1])
                    for j in range(JH):
                        gu_step(j, ws[j], subs[1][0], subs[1][1])
                    cnt["whold"] = None
                for j in range(JH, NJ):
                    w = wnext(wgu[f, j], 2048, (2, KC, 128))
                    for (c0, c1, typ) in subs:
                        gu_step(j, w, c0, c1)
                preload(AF.Sqrt, 0)
                for m in range(KC):
                    w0 = wnext(wd[f, m, 0], 1408, (11, 128))
                    w1 = wnext(wd[f, m, 1], 1408, (11, 128))
                    for (c0, c1, typ) in subs:
                        n = c1 - c0
                        y = y_bank()
                        for hf, w in ((0, w0), (1, w1)):
                            for jj in range(11):
                                mm(y(slice(0, n)), w(jj), aT(hf * 11 + jj, slice(c0, c1)),
                                   hf == 0 and jj == 0, hf == 1 and jj == 10)
                        stt(xT(m, slice(c0, c1)), y(slice(0, n)), 0.5, xT(m, slice(c0, c1)), ALU.mult, ALU.add)

            def proj(w, c0, c1, bank):
                n = c1 - c0
                for k in range(KC):
                    mm(bank(slice(0, n)), w(k), hT(k, slice(c0, c1)), k == 0, k == KC - 1)

            def rope(A_, B_, c0, c1, dst):
                n = c1 - c0
                tt(rp(0, slice(0, n)), A_(slice(0, n)), cosT(slice(c0, c1)), ALU.mult)
                tt(rp(1, slice(0, n)), B_(slice(0, n)), sinT(slice(c0, c1)), ALU.mult)
                tt(dst, rp(0, slice(0, n)), rp(1, slice(0, n)), ALU.add)

            def keep32(g, which, src_tmp, c0, c1):
                rngs = []
                if g is GROUPS[2]:
                    rngs.append((128, 256, 0))
                    rngs.append((256, 320, 128))
                for (lo, hi, off) in rngs:
                    a, b = max(lo, c0), min(hi, c1)
                    if a < b:
                        copy(kv32(which, slice(off + a - lo, off + b - lo)), src_tmp(slice(a - c0, b - c0)), eng="dve")

            def mixer(g):
                dma("sp", cos_t[:, 0:g.n], cosd[:, g.lo:g.lo + g.n], [], [cosT(slice(0, g.n))], "cos")
                dma("sp", sin_t[:, 0:g.n], sind[:, g.lo:g.lo + g.n], [], [sinT(slice(0, g.n))], "sin")
                norm(g, 0, g.n, 1, after=AF.Exp)
                subs = subtiles(g, 0, g.n)
                for hp in range(4):
                    wq = wnext(wmi[hp], 1024, (KC, 128))
                    wp = wnext(wmi[4 + hp], 1024, (KC, 128))
                    for (c0, c1, typ) in subs:
                        A_, B_ = gu_pair()
                        proj(wq, c0, c1, A_)
                        proj(wp, c0, c1, B_)
                        rope(A_, B_, c0, c1, qT(hp, slice(c0, c1)))
                wk = wnext(wmi[8], 1024, (KC, 128))
                wkp = wnext(wmi[9], 1024, (KC, 128))
                for (c0, c1, typ) in subs:
                    n = c1 - c0
                    A_, B_ = gu_pair()
                    proj(wk, c0, c1, A_)
                    proj(wkp, c0, c1, B_)
                    rope(A_, B_, c0, c1, rp(2, slice(0, n)))
                    copy(kT(slice(128 + c0, 128 + c1)), rp(2, slice(0, n)), eng="act")
                    keep32(g, 0, lambda sl_: rp(2, sl_), c0, c1)
                wv = wnext(wmi[10], 1024, (KC, 128))
                for (c0, c1, typ) in subs:
                    n = c1 - c0
                    A_ = gu_one()
                    proj(wv, c0, c1, A_)
                    copy(rp(3, slice(0, n)), A_(slice(0, n)), eng="act")
                    copy(vT(slice(c0, c1)), rp(3, slice(0, n)), eng="dve")
                    keep32(g, 1, lambda sl_: rp(3, sl_), c0, c1)
                for i in range(4):
                    wc = wnext(wmi[11 + i], 1024, (KC, 128))
                    wh = wnext(wmi[15 + i], 1024, (KC, 128))
                    for (c0, c1, typ) in subs:
                        n = c1 - c0
                        A_, B_ = gu_pair()
                        proj(wc, c0, c1, A_)
                        proj(wh, c0, c1, B_)
                        copy(rp(0, slice(0, n)), A_(slice(0, n)), eng="act")
                        dst = uT(i, slice(2 + c0, 2 + c1)) if typ == 'p' else usx(i, slice(32, 96))
                        tt(dst, rp(0, slice(0, n)), B_(slice(0, n)), ALU.mult)
                if g is not GROUPS[0]:
                    copy(uT(slice(0, 4), slice(0, 2)), ucar(), eng="dve")
                dbg_dump(f"g{GROUPS.index(g)}.proj")
                attention(g)
                dbg_dump(f"g{GROUPS.index(g)}.att")
                for i in range(4):
                    wb = wnext(wmi[19 + i], 1024, (KC, 128))
                    for (c0, c1, typ) in subtiles(g, g.noA, g.n):
                        n = c1 - c0
                        A_ = gu_one()
                        proj(wb, c0, c1, A_)
                        cvt = rp(1, slice(0, n))
                        if typ == 'p':
                            u0, u1, u2 = (uT(i, slice(c0 + d, c1 + d)) for d in range(3))
                        else:
                            u0, u1, u2 = (usx(i, slice(16 * d, 16 * d + 64)) for d in range(3))
                        ts(cvt, u0, cs(C_CONVW + 0 * 4 + i), None, ALU.mult)
                        stt(cvt, u1, cs(C_CONVW + 1 * 4 + i), cvt, ALU.mult, ALU.add)
                        stt(cvt, u2, cs(C_CONVW + 2 * 4 + i), cvt, ALU.mult, ALU.add)
                        tt(gcv(i, slice(c0, c1)), cvt, A_(slice(0, n)), ALU.mult)
                preload(AF.Sqrt, 0)
                for m in range(KC):
                    w = wnext(wmo[m], 1024, (KC, 128))
                    for (c0, c1, typ) in subtiles(g, g.noA, g.n):
                        n = c1 - c0
                        y = y_bank()
                        for kc in range(KC):
                            rhs = attT(kc, slice(c0, c1)) if kc < 4 else gcv(kc - 4, slice(c0, c1))
                            mm(y(slice(0, n)), w(kc), rhs, kc == 0, kc == KC - 1)
                        tt(xT(m, slice(c0, c1)), y(slice(0, n)), xT(m, slice(c0, c1)), ALU.add)
                dbg_dump(f"g{GROUPS.index(g)}.mo")
                if g is GROUPS[2]:
                    conv_outputs(g)
                else:
                    e = g.pend
                    copy(ucar(), uT(slice(0, 4), slice(2 + e - 2, 2 + e)), eng="dve")
                    copy(kT(slice(0, 128)), kT(slice(128 + e - 128, 128 + e)), eng="dve")
                    copy(Vtok(0), Vtok(e // 128), eng="dve")

            def softmax_pv_common(sbank, width, mask_acc, sink_acc, slot):
                s_ = s_sc(slot, slice(0, width))
                stt(s_, sbank(slice(0, width)), 0.125, mask_acc, ALU.mult, ALU.add)
                mx = stat(slot, slice(0, 1))
                negm = stat(slot, slice(1, 2))
                rsum = stat(slot, slice(2, 3))
                esk = stat(slot, slice(3, 4))
                den = stat(slot, slice(4, 5))
                rden = stat(slot, slice(5, 6))
                OP("dve", lambda e: e.tensor_reduce(out=mx.ap, in_=s_.ap, axis=AX.X, op=ALU.max), [s_], [mx])
                ts(negm, mx, sink_acc, -1.0, ALU.max, ALU.mult)
                pe_ = pexp(slot, slice(0, width))
                act(pe_, s_, AF.Exp, bias=negm, scale=1.0, accum=rsum)
                act(esk, sink_acc, AF.Exp, bias=negm, scale=1.0)
                tt(den, rsum, esk, ALU.add)
                OP("dve", lambda e: e.reciprocal(out=rden.ap, in_=den.ap), [den], [rden])
                act(pn(slot, slice(0, width)), pe_, AF.Copy, scale=rden)

            def attention(g):
                for c0 in range(0, g.pend, 128):
                    bi = c0 // 128 + 1
                    tr(tpb(slice(0, 128)), vT(slice(c0, c0 + 128)), identb())
                    copy(Vtok(bi), tpb(slice(0, 128)))
                start = g.B[0] if g.B else 0
                batches = []
                for c0 in range(start, g.pend, 128):
                    if g.B and c0 == g.B[0]:
                        mi = 1
                    elif g is GROUPS[0] and c0 == g.M[0]:
                        mi = 2
                    else:
                        mi = 0
                    for kvh in range(2):
                        batches.append((c0, kvh, mi))
                obs = {}

                def S_mm(i):
                    c0, kvh, mi = batches[i]
                    sl = i % 2
                    ph = (64 * kvh, 64 * kvh + 64)
                    for hp in range(4):
                        bank = gu[2 * sl + hp // 2]
                        reg = bank(slice((hp % 2) * 256, (hp % 2) * 256 + 256))
                        mm(reg, identb(), maskb(mi), True, False)
                        mm(reg, qT(hp, slice(c0, c0 + 128), p=ph), kT(slice(c0, c0 + 256), p=ph), False, True)

                def max_min(i):
                    c0, kvh, mi = batches[i]
                    sl = i % 2
                    Sall = Acc(gu2b[sl][:, :], gu[2 * sl]().r + gu[2 * sl + 1]().r)
                    mx = st4(sl, slice(0, 1))
                    OP("dve", lambda e: e.tensor_reduce(out=mx.ap, in_=Sall.ap, axis=AX.X, op=ALU.max), [Sall], [mx])
                    ts(st4(sl, slice(1, 2)), mx, -0.125, nsm(slice(kvh, kvh + 1)), ALU.mult, ALU.min)

                def exps(i):
                    c0, kvh, mi = batches[i]
                    sl = i % 2
                    negm = st4(sl, slice(1, 2))
                    for hp in range(4):
                        bank = gu[2 * sl + hp // 2]
                        act(pb4(sl, hp), bank(slice((hp % 2) * 256, (hp % 2) * 256 + 256)), AF.Exp, bias=negm, scale=0.125,
                            accum=st4(sl, slice(4 + hp, 5 + hp)))
                    act(st4(sl, slice(8, 12)), cst(slice(C_SINK + 4 * kvh, C_SINK + 4 * kvh + 4)), AF.Exp, bias=negm, scale=1.0)

                def den_pn(i):
                    sl = i % 2
                    den = st4(sl, slice(12, 16))
                    tt(den, st4(sl, slice(4, 8)), st4(sl, slice(8, 12)), ALU.add)
                    OP("dve", lambda e: e.reciprocal(out=den.ap, in_=den.ap), [den], [den])
                    for hp in range(4):
                        o_, i_, sc_ = pn4(sl, hp), pb4(sl, hp), st4(sl, slice(12 + hp, 13 + hp))
                        if hp % 2 == 0:
                            act(o_, i_, AF.Copy, scale=sc_)
                        else:
                            ts(o_, i_, sc_, None, ALU.mult)

                def trs(i):
                    sl = i % 2
                    px = ptx[sl]
                    for hp in range(4):
                        for hf in range(2):
                            tr(px(2 * hp + hf), pn4(sl, hp, slice(hf * 128, hf * 128 + 128)), identb())

                def pT_copy(i):
                    sl = i % 2
                    copy(pT4(sl), ptx[sl](), eng="dve")

                def PV(i):
                    c0, kvh, mi = batches[i]
                    sl = i % 2
                    bi = c0 // 128 + 1
                    ph = (64 * kvh, 64 * kvh + 64)
                    if kvh == 0:
                        obs[c0] = y_bank()
                    ob = obs[c0]
                    for hp in range(4):
                        oreg = ob(slice(hp * 128, hp * 128 + 128), p=ph)
                        mm(oreg, Vtok(bi - 1, slice(kvh * 64, kvh * 64 + 64)), pT4(sl, 2 * hp), True, False)
                        mm(oreg, Vtok(bi, slice(kvh * 64, kvh * 64 + 64)), pT4(sl, 2 * hp + 1), False, True)

                def att_copy(i):
                    c0, kvh, mi = batches[i]
                    if kvh == 1:
                        ob = obs[c0]
                        src = LT(ob.phys, ob.ap.rearrange("p (a b) -> p a b", a=4), 0, 4, (4, 128))
                        copy(attT(slice(0, 4), slice(c0, c0 + 128)), src(), eng="act")

                nb = len(batches)
                for it in range(nb + 2):
                    a, b_, c_ = it, it - 1, it - 2
                    if 0 <= c_ < nb:
                        trs(c_)
                        pT_copy(c_)
                    if a < nb:
                        S_mm(a)
                    if 0 <= c_ < nb:
                        PV(c_)
                    if 0 <= b_ < nb:
                        exps(b_)
                    if 0 <= b_ < nb:
                        den_pn(b_)
                    if a < nb:
                        max_min(a)
                    if 0 <= c_ < nb:
                        att_copy(c_)
                dbg_dump(f"g{GROUPS.index(g)}.attp")
                if g.S:
                    sample_attention(g)
                dbg_dump(f"g{GROUPS.index(g)}.atts")
                if g is GROUPS[2]:
                    kv_outputs(g)

            def sample_attention(g):
                S0 = g.S[0]
                ksrc = kT_t[:, 128 + S0:128 + S0 + 64].rearrange("p (t o i) -> p o t i", t=4, o=2, i=8)
                vsrc = vT_t[:, S0:S0 + 64].rearrange("p (t o i) -> p o t i", t=4, o=2, i=8)
                kdst = knc_t[:].rearrange("p o (t i) -> p o t i", t=4)
                vdst = vnc_t[:].rearrange("p o (t i) -> p o t i", t=4)
                OP("dve", lambda e: e.tensor_copy(out=kdst, in_=ksrc), [kT(slice(128 + S0, 128 + S0 + 64))], [knc.whole()])
                OP("dve", lambda e: e.tensor_copy(out=vdst, in_=vsrc), [vT(slice(S0, S0 + 64))], [vnc.whole()])
                OP("dve", lambda e: e.memset(vno_t[:], 0.0), [], [vno.whole()])
                for o in range(2):
                    tr(tpb(slice(0, 128), p=(0, 32)), vnc(o), identb())
                    copy(vno(o, p=(0, 32)), tpb(slice(0, 128), p=(0, 32)))
                dbg_dump("g2.sa3")
                OP("dve", lambda e: e.memset(ovl_t[:, 40992 // 2: 40992 // 2 + 2048], 0.0), [], [qpad.whole()])
                for gq in range(4):
                    for i in range(8):
                        srcap = qT.ap[:, gq, S0:S0 + 64].rearrange("p (t o i) -> p i o t", t=4, o=2, i=8)[:, i]
                        dstap = qpad.ap.rearrange("p (o i) c -> p i o c", o=2)[:, i, :, 16 * i + 4 * gq:16 * i + 4 * gq + 4]
                        rd = [qT(gq, slice(S0, S0 + 64))]
                        wrr = [qpad(i, slice(16 * i + 4 * gq, 16 * i + 4 * gq + 4)),
                               qpad(8 + i, slice(16 * i + 4 * gq, 16 * i + 4 * gq + 4))]
                        if (gq * 8 + i) % 2 == 0:
                            OP("dve", lambda e, d=dstap, s=srcap: e.tensor_copy(out=d, in_=s), rd, wrr)
                        else:
                            OP("act", lambda e, d=dstap, s=srcap: e.copy(out=d, in_=s), rd, wrr)
                dbg_dump("g2.sa4")
                mask = cst(slice(C_MASKS, C_MASKS + 160))
                for kvh in range(2):
                    ph = (64 * kvh, 64 * kvh + 64)
                    for o in range(2):
                        slot = cnt["att"] % 2
                        sbk = gu[slot]
                        pb = ptb[slot]
                        cnt["att"] += 1
                        for i in range(8):
                            mm(sbk(slice(0, 128)), qpad(8 * o + i, p=ph), kTs(8 * o + i, p=ph), i == 0, i == 7)
                        for i in range(8):
                            mm(sbk(slice(128, 160)), qpad(8 * o + i, p=ph), knc(o, p=ph), i == 0, i == 7)
                        softmax_pv_common(sbk, 160, mask, cs(C_SINKS + kvh), slot)
                        dbg_dump("g2.sa5")
                        tr(pb(slice(0, 128)), pn(slot, slice(0, 128)), identb())
                        tr(pb(slice(128, 256), p=(0, 32)), pn(slot, slice(128, 160)), identb())
                        copy(pT(slot, slice(0, 128)), pb(slice(0, 128)))
                        OP("dve", lambda e, sl_=slot: e.memset(pT_t[:, sl_, 128:256], 0.0), [], [pT(slot, slice(128, 256))])
                        copy(pT(slot, slice(128, 256), p=(0, 32)), pb(slice(128, 256), p=(0, 32)))
                        dbg_dump("g2.sa6")
                        ob = y_bank()
                        for i in range(8):
                            oreg = ob(slice(16 * i, 16 * i + 16), p=ph)
                            mm(oreg, Vs(8 * o + i, slice(kvh * 64, kvh * 64 + 64)), pT(slot, slice(16 * i, 16 * i + 16)), True, False)
                            mm(oreg, vno(o, slice(kvh * 64, kvh * 64 + 64)),
                               pT(slot, slice(128 + 16 * i, 128 + 16 * i + 16)), False, True)
                        dbg_dump("g2.sa7")
                        dstap = attT.ap[ph[0]:ph[1], :, S0:S0 + 64].rearrange("p g (t o i) -> p o i g t", t=4, o=2, i=8)[:, o]
                        srcap = ob.ap[ph[0]:ph[1], 0:128].rearrange("p (i g t) -> p i g t", i=8, g=4, t=4)
                        OP("dve", lambda e, d=dstap, s=srcap: e.tensor_copy(out=d, in_=s),
                           [ob(slice(0, 128))], [attT(slice(0, 4), slice(S0, S0 + 64))])

            def kv_outputs(g):
                sl = io_slot()
                for w_ in range(2):
                    tr(tp(slice(w_ * 128, w_ * 128 + 128)), kv32(w_, slice(0, 128)), ident())
                copy(stg(sl, slice(0, 256)), tp(slice(0, 256)))
                dma("sp", kpd[:, :], stg_t[:, sl, 0:128], [stg(sl)], [], f"io{sl}", is_out=True)
                dma("sp", vpd[:, :], stg_t[:, sl, 128:256], [stg(sl)], [], f"io{sl}", is_out=True)
                sl = io_slot()
                for w_ in range(2):
                    tr(tp(slice(256 + w_ * 128, 256 + w_ * 128 + 128), p=(0, 64)), kv32(w_, slice(128, 192)), ident())
                copy(stg(sl, slice(0, 256), p=(0, 64)), tp(slice(256, 512), p=(0, 64)))
                for t in range(4):
                    dma("sp", ksd[:, 124 + t, :], stg_t[t * 16:t * 16 + 16, sl, 0:128], [stg(sl)], [], f"io{sl}", is_out=True)
                    dma("sp", vsd[:, 124 + t, :], stg_t[t * 16:t * 16 + 16, sl, 128:256], [stg(sl)], [], f"io{sl}", is_out=True)

            def conv_hist_load(g):
                sl = io_slot()
                for r in range(2):
                    dma("sp", stg_t[r * 16:r * 16 + 16, sl, 0:512], sconvd[:, r, :], [], [stg(sl)], f"io{sl}")
                for i in range(4):
                    tr(tp(slice(i * 32, i * 32 + 32)), stg(sl, slice(i * 128, i * 128 + 128), p=(0, 32)),
                       LT("cst", cst_t[0:32, C_ID:C_ID + 32], C_ID * 4, 4, (32,))())
                src = LT("tp", tp_t[:, 0:128].rearrange("p (a b) -> p a b", a=4), 0, 4, (4, 32))
                copy(usx(slice(0, 4), slice(0, 32)), src())

            def conv_outputs(g):
                sl = io_slot()
                for i in range(4):
                    tr(tp(slice(i * 128, i * 128 + 128), p=(0, 32)), usx(i, slice(64, 96)), ident())
                copy(stg(sl, slice(0, 512), p=(0, 32)), tp(slice(0, 512), p=(0, 32)))
                for r in range(2):
                    dma("sp", convsd[:, r, :], stg_t[r * 16:r * 16 + 16, sl, 0:512], [stg(sl)], [], f"io{sl}", is_out=True)
                sl = io_slot()
                e = g.pend
                for i in range(4):
                    tr(tp(slice(i * 128, i * 128 + 128), p=(0, 2)), uT(i, slice(2 + e - 2, 2 + e)), ident())
                copy(stg(sl, slice(0, 512), p=(0, 2)), tp(slice(0, 512), p=(0, 2)))
                dma("sp", convpd[:, :], stg_t[0:2, sl, 0:512], [stg(sl)], [], f"io{sl}", is_out=True)

            hist_slots = []

            def pool_hist_dma(g):
                for part, (r0, r1) in enumerate(((0, 8), (8, 15))):
                    sl = io_slot()
                    hist_slots.append(sl)
                    for r in range(r0, r1):
                        dma("sp", stg_t[(r - r0) * 16:(r - r0) * 16 + 16, sl, :], spoold[:, r, :], [], [stg(sl)], f"io{sl}")

            def pool_hist_tr(g):
                for part, (r0, r1) in enumerate(((0, 8), (8, 15))):
                    sl = hist_slots[part]
                    nr = (r1 - r0) * 16
                    for half in range(2):
                        bk = tbank()
                        for kk in range(4):
                            k = half * 4 + kk
                            tr(bk(slice(kk * 128, kk * 128 + nr)), stg(sl, slice(k * 128, k * 128 + 128), p=(0, nr)),
                               LT("cst", cst_t[0:nr, C_ID:C_ID + nr], C_ID * 4, 4, (nr,))())
                        src = LT(bk.phys, bk.ap.rearrange("p (a b) -> p a b", a=4), 0, 4, (4, 128))
                        copy(hsx(slice(half * 4, half * 4 + 4), slice(r0 * 16, r0 * 16 + nr)), src(slice(0, 4), slice(0, nr)))

            def pool_stage(g):
                gi_n = 4
                norm(g, g.noA, g.n, gi_n, write_h=False)
                wp = wnext(wpl, 2048, (8, 2, 128))
                e = g.pend
                for k in range(KC):
                    stt(h1c(k), xT(k, slice(e - 15, e)), cs(C_GAIN + gi_n * 8 + k), rstd(slice(e - 15, e)), ALU.mult, ALU.mult)
                have_carry = g is not GROUPS[0]
                if g is GROUPS[2]:
                    sl = io_slot()
                    for half in range(2):
                        for kk in range(4):
                            k = half * 4 + kk
                            tr(tp(slice(kk * 128, kk * 128 + 128), p=(0, 15)), h1c(k), ident())
                        copy(stg(sl, slice(half * 512, half * 512 + 512), p=(0, 15)), tp(slice(0, 512), p=(0, 15)))
                    dma("sp", poolpd[:, :], stg_t[0:15, sl, :], [stg(sl)], [], f"io{sl}", is_out=True)
                psubs = subtiles(g, g.out, g.n)
                assert sum(1 for s_ in psubs if s_[2] == 'p') <= 2
                for si, (c0, c1, typ) in enumerate(psubs):
                    if typ == 'p' and si > 0:
                        for k in range(KC):
                            stt(h1b(k), xT(k, slice(c0 - 15, c0)), cs(C_GAIN + gi_n * 8 + k), rstd(slice(c0 - 15, c0)), ALU.mult, ALU.mult)
                for gi in range(4):
                    win = 2 << gi
                    ts(Dg(gi, 0), identb(), 1.0 / win - 1.0, None, ALU.mult)
                    ts(Dg(gi, 1), identb(), 1.0 / win, None, ALU.mult)
                for si, (c0, c1, typ) in enumerate(psubs):
                    n = c1 - c0
                    if typ == 'p':
                        W = 15 + n
                        for k in range(KC):
                            gk = cs(C_GAIN + gi_n * 8 + k)
                            if si == 0 and c0 >= 15:
                                stt(h1bf(k, slice(0, W)), xT(k, slice(c0 - 15, c1)), gk, rstd(slice(c0 - 15, c1)), ALU.mult, ALU.mult)
                            else:
                                if si == 0:
                                    assert c0 == 0
                                    copy(h1bf(k, slice(0, 15)), h1carry_prev(k), eng="act")
                                else:
                                    copy(h1bf(k, slice(0, 15)), h1b(k), eng="act")
                                stt(h1bf(k, slice(15, W)), xT(k, slice(c0, c1)), gk, rstd(slice(c0, c1)), ALU.mult, ALU.mult)

                        def p_stage(gi):
                            win = 2 << gi
                            for kc in range(2):
                                k = 2 * gi + kc
                                pbk = gu[cnt["pb"] % 4]
                                cnt["pb"] += 1
                                for i in range(win):
                                    mm(pbk(slice(0, n)), Dg(gi, 0 if i == 0 else 1), h1bf(k, slice(15 - i, 15 - i + n)), i == 0, i == win - 1)
                                copy(ppT(k, slice(0, n)), pbk(slice(0, n)), eng="act")

                        def z_stage(gi):
                            for e_ in range(2):
                                z = y_bank()
                                for kc in range(2):
                                    mm(z(slice(0, n)), wp(gi * 2 + e_, kc), ppT(2 * gi + kc, slice(0, n)), kc == 0, kc == 1)
                                m = 2 * gi + e_
                                stt(xT(m, slice(c0, c1)), z(slice(0, n)), cs(C_PSC + m), xT(m, slice(c0, c1)), ALU.mult, ALU.add)

                        for gi in range(5):
                            if gi < 4:
                                p_stage(gi)
                            if gi >= 1:
                                z_stage(gi - 1)
                        continue
                    step, W = 16, 19
                    for gi in range(4):
                        win = 2 << gi
                        for kc in range(2):
                            k = 2 * gi + kc
                            gk = cs(C_GAIN + gi_n * 8 + k)
                            stt(hsx(k, slice(240, 304)), xT(k, slice(c0, c1)), gk, rstd(slice(c0, c1)), ALU.mult, ALU.mult)
                            H = lambda a, b, k=k: hsx(k, slice(a * 16, b * 16))
                            P = lambda idx, a, b: pa[idx](slice(a * step, b * step))
                            if gi == 0:
                                tt(P(0, 14, W - 1), H(15, W), H(14, W - 1), ALU.add)
                                wsum = P(0, 14, W - 1)
                            else:
                                tt(P(0, 0, W - 1), H(1, W), H(0, W - 1), ALU.add)
                            if gi == 1:
                                tt(P(1, 12, W - 3), P(0, 14, W - 1), P(0, 12, W - 3), ALU.add)
                                wsum = P(1, 12, W - 3)
                            elif gi > 1:
                                tt(P(1, 0, W - 3), P(0, 2, W - 1), P(0, 0, W - 3), ALU.add)
                            if gi == 2:
                                tt(P(2, 8, W - 7), P(1, 12, W - 3), P(1, 8, W - 7), ALU.add)
                                wsum = P(2, 8, W - 7)
                            elif gi > 2:
                                tt(P(2, 0, W - 7), P(1, 4, W - 3), P(1, 0, W - 7), ALU.add)
                                tt(P(3, 0, W - 15), P(2, 8, W - 7), P(2, 0, W - 15), ALU.add)
                                wsum = P(3, 0, W - 15)
                            stt(ppT(k, slice(0, n)), wsum, 1.0 / win, H(15, W), ALU.mult, ALU.subtract)
                        for e_ in range(2):
                            z = y_bank()
                            for kc in range(2):
                                mm(z(slice(0, n)), wp(gi * 2 + e_, kc), ppT(2 * gi + kc, slice(0, n)), kc == 0, kc == 1)
                            m = 2 * gi + e_
                            stt(xT(m, slice(c0, c1)), z(slice(0, n)), cs(C_PSC + m), xT(m, slice(c0, c1)), ALU.mult, ALU.add)
                    pool_sample_out(g)

            h1prev_t = None

            def h1carry_prev(k):
                return h1p(k)

            def pool_sample_out(g):
                for part, (r0, r1) in enumerate(((4, 12), (12, 19))):
                    sl = io_slot()
                    nr = (r1 - r0) * 16
                    for half in range(2):
                        for kk in range(4):
                            k = half * 4 + kk
                            tr(tp(slice(kk * 128, kk * 128 + 128), p=(0, nr)), hsx(k, slice(r0 * 16, r1 * 16)), ident())
                        copy(stg(sl, slice(half * 512, half * 512 + 512), p=(0, nr)), tp(slice(0, 512), p=(0, nr)))
                    for r in range(r0, r1):
                        dma("sp", poolsd[:, r - 4, :], stg_t[(r - r0) * 16:(r - r0) * 16 + 16, sl, :], [stg(sl)], [], f"io{sl}", is_out=True)

            def final_out(g):
                norm(g, g.out, g.n, 6, write_h=False)
                for (c0, c1, typ) in subtiles(g, g.out, g.n):
                    for k in range(KC):
                        stt(xT(k, slice(c0, c1)), xT(k, slice(c0, c1)), cs(C_GAIN + 6 * 8 + k), rstd(slice(c0, c1)), ALU.mult, ALU.mult)
                    for b0 in range(c0, c1, 128):
                        n = min(128, c1 - b0)
                        sl = cnt["fo"] % 8
                        cnt["fo"] += 1
                        for half in range(2):
                            bk = tbank()
                            for kk in range(4):
                                k = half * 4 + kk
                                tr(bk(slice(kk * 128, kk * 128 + 128), p=(0, n)), xT(k, slice(b0, b0 + n)), ident())
                            copy(lstg(sl, slice(half * 512, half * 512 + 512), p=(0, n)), bk(slice(0, 512), p=(0, n)))
                        if typ == 'p':
                            row = g.lo + b0 - 256
                            dma("sp", y_main[row:row + n, :], lstg.ap[0:n, sl, :], [lstg(sl)], [], f"ld{sl}", is_out=True)
                        else:
                            dma("sp", y_samp[:, :], lstg.ap[0:n, sl, :], [lstg(sl)], [], f"ld{sl}", is_out=True)

            def dbg_dump(tag):
                if dbg_point == tag:
                    dma("sp", dbgd[:, :], xT_t[:].rearrange("p a b -> p (a b)"), [xT()], [], "dbg", is_out=True)
                    raise _Stop()

            h1p_t = sb_h1p[0]
            h1p = LT("h1p", h1p_t[:], 0, 4, (8, 15))

            def sample_cache_prep():
                dma("pool", ovl_t[:, 45088 // 2: 45088 // 2 + 2048].rearrange("p (b c) -> p b c", b=16),
                    ckd.rearrange("b t c -> t b c"), [], [kctok.whole()], "kc")
                dma("pool", ovl_t[:, 49184 // 2: 49184 // 2 + 2048].rearrange("p (b c) -> p b c", b=16),
                    cvd.rearrange("b t c -> t b c"), [], [Vs.whole()], "vc")
                dma("sp", ksd[:, 0:124, :], ckd[:, 4:128, :], [], [], "cpk", is_out=True)
                dma("sp", vsd[:, 0:124, :], cvd[:, 4:128, :], [], [], "cpv", is_out=True)
                for r in range(2):
                    for bb in range(8):
                        b = r * 8 + bb
                        tr(tpb(slice(bb * 128, bb * 128 + 128)), kctok(b), identb())
                    src = LT("tpb", tpb_t[:].rearrange("p (a b) -> p a b", a=8), 0, 2, (8, 128))
                    copy(kTs(slice(r * 8, r * 8 + 8)), src())

            sample_cache_prep()
            for gi_, g in enumerate(GROUPS):
              try:
                load_x(g)
                dbg_dump(f"g{gi_}.load")
                if g.S:
                    conv_hist_load(g)
                ffn(g, 0, g.n, 0, 0)
                dbg_dump(f"g{gi_}.ffn0")
                mixer(g)
                if g.S:
                    pool_hist_dma(g)
                dbg_dump(f"g{gi_}.mix")
                ffn(g, g.noA, g.n, 1, 2)
                if g.S:
                    pool_hist_tr(g)
                dbg_dump(f"g{gi_}.ffn1")
                ffn(g, g.noA, g.n, 2, 3)
                dbg_dump(f"g{gi_}.ffn2")
                pool_stage(g)
                if g is not GROUPS[2]:
                    copy(h1p(), h1c(), eng="dve")
                dbg_dump(f"g{gi_}.pool")
                ffn(g, g.out, g.n, 3, 5)
                dbg_dump(f"g{gi_}.ffn3")
                final_out(g)
              except _Stop:
                break

        sb_h1p = [sb("h1p", [128, 8, 15], F32)]

        wspecs = []
        emit(Sched(True), wspecs)
        S = Sched(False)
        emit(S, wspecs)
        S.finalize()

        ops = S.ops
        engs = ["pe", "act", "dve", "pool", "sp"]
        sigval = {}
        c = {e: 0 for e in engs}
        for i, o in enumerate(ops):
            if o["signal"]:
                c[o["eng"]] += 1
                sigval[i] = c[o["eng"]]
        dma_keys = sorted({o["dma"] for o in ops if o["dma"] is not None})
        sems = {}
        for e in engs:
            sems[("eng", e)] = es.enter_context(nc.semaphore(f"s_{e}"))
        for k in dma_keys:
            sems[("dma", k)] = es.enter_context(nc.semaphore(f"d_{k}"))
        block = es.enter_context(nc.Block())

        def run_engine(ename, e):
            waited = {}
            for i, o in enumerate(ops):
                if o["eng"] != ename:
                    continue
                for k, v in o["waits"].items():
                    val = v if k[0] == "dma" else sigval[v]
                    if waited.get(k, 0) >= val:
                        continue
                    e.wait_ge(sems[k], val)
                    waited[k] = val
                if o["fn"] is None:
                    continue
                inst = o["fn"](e)
                if o["dma"] is not None:
                    inst.then_inc(sems[("dma", o["dma"])], 16)
                elif o["signal"]:
                    inst.then_inc(sems[("eng", ename)], 1)

        @block.tensor
        def _(e):
            run_engine("pe", e)

        @block.scalar
        def _(e):
            run_engine("act", e)

        @block.vector
        def _(e):
            run_engine("dve", e)

        @block.gpsimd
        def _(e):
            run_engine("pool", e)

        @block.sync
        def _(e):
            run_engine("sp", e)

    return nc


_CACHE = {}


def _prep_weights(ffn_w_gate, ffn_w_up, ffn_w_down, mix_w_in, mix_w_out, pool_w):
    f32 = np.float32
    wgu = np.empty((4, NJ, 128, 2, KC, 128), f32)
    wd = np.empty((4, 8, 2, 128, 11, 128), f32)
    for l in range(2):
        for i in range(2):
            f = l * 2 + i
            g = np.asarray(ffn_w_gate[l, i]).reshape(KC, 128, NJ, 128).transpose(2, 1, 0, 3)
            u = np.asarray(ffn_w_up[l, i]).reshape(KC, 128, NJ, 128).transpose(2, 1, 0, 3)
            wgu[f, :, :, 0] = g
            wgu[f, :, :, 1] = u
            dn = np.asarray(ffn_w_down[l, i]).reshape(2, 11, 128, 8, 128).transpose(3, 0, 2, 1, 4)
            wd[f] = dn
    wgu = wgu.reshape(4, NJ, 128, 2048)
    wd = wd.reshape(4, 8, 2, 128, 1408)
    W = np.asarray(mix_w_in[0])
    tiles = []
    r = np.arange

    def head(h):
        return r(h * 64, h * 64 + 64)

    def headp(h):
        return np.concatenate([r(h * 64 + 32, h * 64 + 64), r(h * 64, h * 64 + 32)])

    for hp in range(4):
        tiles.append(np.concatenate([head(hp), head(hp + 4)]))
    for hp in range(4):
        tiles.append(np.concatenate([headp(hp), headp(hp + 4)]))
    tiles.append(512 + r(128))
    tiles.append(512 + np.concatenate([headp(0), headp(1)]))
    tiles.append(640 + r(128))
    for i in range(4):
        tiles.append(1280 + i * 128 + r(128))
    for i in range(4):
        tiles.append(1792 + i * 128 + r(128))
    for i in range(4):
        tiles.append(768 + i * 128 + r(128))
    wmi = np.stack([W[:, t].reshape(KC, 128, 128).transpose(1, 0, 2).reshape(128, 1024) for t in tiles]).astype(f32)
    rows = []
    for hp in range(4):
        rows += [head(hp), head(hp + 4)]
    rows.append(512 + r(512))
    rows = np.concatenate(rows)
    Wo = np.asarray(mix_w_out[0])[rows, :]
    wmo = Wo.reshape(KC, 128, 8, 128).transpose(2, 1, 0, 3).reshape(8, 128, 1024).astype(f32)
    wpl = np.asarray(pool_w[0]).reshape(4, 2, 128, 2, 128).transpose(2, 0, 3, 1, 4).reshape(128, 2048).astype(f32)
    return dict(wgu=np.ascontiguousarray(wgu), wd=np.ascontiguousarray(wd), wmi=np.ascontiguousarray(wmi),
                wmo=np.ascontiguousarray(wmo), wpl=np.ascontiguousarray(wpl))


def _core_inputs(c, x_prompt, x_sample, cache_k, cache_v, state_conv, state_pool, meta_tokens, ln_gain,
                 attn_sink, conv_w, pool_scale, final_gain):
    f32 = np.float32
    b, half = c // 2, c % 2
    xin = np.zeros((TCORE, D), f32)
    pos = np.zeros(TCORE, np.int64)
    if half == 0:
        xin[240:256] = meta_tokens
        pos[128:256] = np.arange(128) - 112
        xin[256:2304] = x_prompt[b, 0:2048]
        pos[256:2304] = 16 + np.arange(2048)
    else:
        xin[0:256] = x_prompt[b, 1792:2048]
        pos[0:256] = 16 + 1792 + np.arange(256)
        xin[256:2304] = x_prompt[b, 2048:4096]
        pos[256:2304] = 16 + 2048 + np.arange(2048)
    xs = x_sample[16 * c:16 * c + 16]
    xin[2304:2368] = xs.transpose(1, 0, 2).reshape(64, D)
    pos[2304:2368] = 8192 + np.repeat(np.arange(4), 16)
    inv = (np.float32(10000.0) ** (-np.arange(32, dtype=f32) / np.float32(32))).astype(f32)
    ang = pos.astype(f32)[:, None] * inv[None, :]
    cosv = np.cos(ang).astype(f32).T
    sinv = np.sin(ang).astype(f32).T
    cosd = np.concatenate([cosv, cosv, cosv, cosv], 0)
    sind = np.concatenate([-sinv, sinv, -sinv, sinv], 0)
    cst = np.zeros((128, NCST), f32)
    gl = [ln_gain[0, 0], ln_gain[0, 1], ln_gain[0, 2], ln_gain[1, 0], ln_gain[1, 1], ln_gain[1, 2], final_gain]
    for gi, gv in enumerate(gl):
        cst[:, C_GAIN + gi * 8:C_GAIN + gi * 8 + 8] = np.asarray(gv).reshape(8, 128).T
    for j in range(3):
        cst[:, C_CONVW + j * 4:C_CONVW + j * 4 + 4] = np.asarray(conv_w[0, j]).reshape(4, 128).T
    cst[:, C_PSC:C_PSC + 8] = np.asarray(pool_scale[0]).reshape(8, 128).T
    cst[:, C_SINK:C_SINK + 8] = np.asarray(attn_sink[0]).reshape(1, 8)
    rr = np.arange(128)
    for kvh in range(2):
        cst[:, C_SINKS + kvh] = np.asarray(attn_sink[0, kvh])[(rr % 16) // 4]
    i_ = np.arange(128)[:, None]
    j_ = np.arange(256)[None, :]
    std = (j_ >= i_) & (j_ <= i_ + 128)
    mB = std.copy()
    mM0 = std.copy()
    if half == 0:
        mB &= (j_ >= 128 + 112)
        mM0 &= (j_ >= 112)
    for mi, m in enumerate((std, mB, mM0)):
        cst[:, C_MASK + mi * 256:C_MASK + mi * 256 + 256] = np.where(m, 0.0, NEG)
    tok = (rr % 4)[:, None]
    ii = (rr // 16)[:, None]
    jc = np.arange(128)[None, :]
    mc = jc >= tok
    cn = np.arange(32)[None, :]
    mn = ((cn % 8) == ii) & ((cn // 8) <= tok)
    cst[:, C_MASKS:C_MASKS + 160] = np.where(np.concatenate([mc, mn], 1), 0.0, NEG)
    cst[:, C_ID:C_ID + 128] = np.eye(128, dtype=f32)
    return dict(
        xin=xin, cosd=np.ascontiguousarray(cosd), sind=np.ascontiguousarray(sind), cst=cst,
        ck=np.ascontiguousarray(cache_k[0, 16 * c:16 * c + 16].reshape(16, 128, 128)),
        cv=np.ascontiguousarray(cache_v[0, 16 * c:16 * c + 16].reshape(16, 128, 128)),
        sconv=np.ascontiguousarray(state_conv[0, 16 * c:16 * c + 16]),
        spool=np.ascontiguousarray(state_pool[0, 16 * c:16 * c + 16]),
    )


def kernel(x_prompt, x_sample, cache_k, cache_v, state_conv, state_pool, meta_tokens, ln_gain, ffn_w_gate,
           ffn_w_up, ffn_w_down, mix_w_in, attn_sink, conv_w, mix_w_out, pool_w, pool_scale, final_gain,
           _dbg=None, _ncores=8):
    A = lambda v: np.asarray(v, dtype=np.float32)
    x_prompt, x_sample, cache_k, cache_v = A(x_prompt), A(x_sample), A(cache_k), A(cache_v)
    state_conv, state_pool, meta_tokens, ln_gain = A(state_conv), A(state_pool), A(meta_tokens), A(ln_gain)
    attn_sink, conv_w, pool_scale, final_gain = A(attn_sink), A(conv_w), A(pool_scale), A(final_gain)
    wts = _prep_weights(A(ffn_w_gate), A(ffn_w_up), A(ffn_w_down), A(mix_w_in), A(mix_w_out), A(pool_w))
    key = ("nc", _dbg)
    if key not in _CACHE:
        _CACHE[key] = build_program(_dbg)
    nc = _CACHE[key]
    in_maps = []
    for c in range(_ncores):
        m = _core_inputs(c, x_prompt, x_sample, cache_k, cache_v, state_conv, state_pool, meta_tokens, ln_gain,
                         attn_sink, conv_w, pool_scale, final_gain)
        m.update(wts)
        in_maps.append(m)
    res = run_bass_kernel_spmd(nc, in_maps, core_ids=list(range(_ncores)))
    R = res.results
    if _dbg is not None:
        return R
    f32 = np.float32
    y_prompt = np.empty((4, 4096, D), f32)
    y_sample = np.empty((128, 4, D), f32)
    k_p = np.empty((1, 4, 128, 2, 64), f32)
    v_p = np.empty((1, 4, 128, 2, 64), f32)
    conv_p = np.empty((1, 4, 2, 512), f32)
    pool_p = np.empty((1, 4, 15, D), f32)
    k_s = np.empty((1, 128, 128, 2, 64), f32)
    v_s = np.empty((1, 128, 128, 2, 64), f32)
    conv_s = np.empty((1, 128, 2, 512), f32)
    pool_s = np.empty((1, 128, 15, D), f32)
    for c in range(8):
        b, half = c // 2, c % 2
        r = R[c]
        y_prompt[b, half * 2048:(half + 1) * 2048] = r["y_main"]
        y_sample[16 * c:16 * c + 16] = r["y_samp"].reshape(4, 16, D).transpose(1, 0, 2)
        if half == 1:
            k_p[0, b] = r["kp"].reshape(128, 2, 64)
            v_p[0, b] = r["vp"].reshape(128, 2, 64)
            conv_p[0, b] = r["convp"]
            pool_p[0, b] = r["poolp"]
        k_s[0, 16 * c:16 * c + 16] = r["ks"].reshape(16, 128, 2, 64)
        v_s[0, 16 * c:16 * c + 16] = r["vs"].reshape(16, 128, 2, 64)
        conv_s[0, 16 * c:16 * c + 16] = r["convs"]
        pool_s[0, 16 * c:16 * c + 16] = r["pools"]
    return (y_prompt, y_sample, k_p, v_p, conv_p, pool_p, k_s, v_s, conv_s, pool_s)
```

```python
import numpy as np
import concourse.bass as bass
import concourse.mybir as mybir
from concourse.bass_utils import run_bass_kernel_spmd
from contextlib import ExitStack

F32 = mybir.dt.float32
BF16 = mybir.dt.bfloat16
AF = mybir.ActivationFunctionType
ALU = mybir.AluOpType
AX = mybir.AxisListType

D = 1024
KC = 8
NJ = 22
TCORE = 2368
TG = 1024
NSLOT = 6
EPS = 1e-6
NEG = -30000.0
C_GAIN, C_CONVW, C_PSC, C_SINK, C_SINKS, C_MASK, C_MASKS, C_ID = 0, 56, 68, 76, 84, 86, 854, 1014
NCST = 1142


class _Stop(Exception):
    pass


class Grp:
    def __init__(self, lo, n, A, B, M, S):
        self.lo, self.n, self.A, self.B, self.M, self.S = lo, n, A, B, M, S
        self.noA = (B[1] - 16) if B else 0
        self.out = M[0]
        self.pend = M[1]


GROUPS = [
    Grp(0, 1024, (0, 128), (128, 256), (256, 1024), None),
    Grp(1024, 1024, None, None, (0, 1024), None),
    Grp(2048, 320, None, None, (0, 256), (256, 320)),
]


def subtiles_flat(lo, hi):
    out = []
    c = lo
    while c < hi:
        n = min(512, hi - c)
        out.append((c, c + n, 'p'))
        c += n
    return out


def subtiles(g, lo, hi):
    out = []
    pe = min(hi, g.pend)
    c = lo
    while c < pe:
        n = min(512, pe - c)
        out.append((c, c + n, 'p'))
        c += n
    if g.S and hi > g.S[0]:
        out.append((max(lo, g.S[0]), hi, 's'))
    return out


PSUM_NAMES = {"gu0", "gu1", "gu2", "gu3", "yb0", "yb1", "tp", "tpb"}


class Acc:
    __slots__ = ("ap", "r")

    def __init__(self, ap, r):
        self.ap, self.r = ap, r


class LT:
    def __init__(self, phys, ap, base, esize, fshape):
        self.phys, self.ap, self.base, self.esize, self.fshape = phys, ap, base, esize, tuple(fshape)
        st = [1] * len(fshape)
        for i in range(len(fshape) - 2, -1, -1):
            st[i] = st[i + 1] * fshape[i + 1]
        self.st = st

    def __call__(self, *idx, p=None):
        idx = list(idx) + [slice(None)] * (len(self.fshape) - len(idx))
        key = (slice(p[0], p[1]) if p else slice(None),) + tuple(idx)
        ap = self.ap[key]
        offs = [0]
        for d in range(len(idx) - 1):
            ix = idx[d]
            if isinstance(ix, int):
                offs = [o + ix * self.st[d] for o in offs]
            else:
                lo, hi, step = ix.indices(self.fshape[d])
                offs = [o + i * self.st[d] for o in offs for i in range(lo, hi, step)]
        ix = idx[-1]
        if isinstance(ix, int):
            lo, hi = ix, ix + 1
        else:
            lo, hi, step = ix.indices(self.fshape[-1])
        rs = [(self.phys, self.base + (o + lo) * self.esize, self.base + (o + hi) * self.esize) for o in offs]
        return Acc(ap, rs)

    def whole(self):
        n = 1
        for s in self.fshape:
            n *= s
        return [(self.phys, self.base, self.base + n * self.esize)]


class Sched:
    def __init__(self, dry):
        self.dry = dry
        self.ops = []
        self.recs = {}
        self.sem_total = {}
        self.final_deps = []
        self.bank_last = {}

    def op(self, eng, fn, reads=(), writes=(), dma=None, is_out=False):
        if self.dry:
            return None
        oid = len(self.ops)
        deps = set()
        for (ph, lo, hi) in reads:
            for r in self.recs.get(ph, ()):
                if r[3] and r[0] < hi and lo < r[1]:
                    deps.add(r[2])
        for (ph, lo, hi) in writes:
            for r in self.recs.get(ph, ()):
                if r[0] < hi and lo < r[1]:
                    deps.add(r[2])
        banks = {ph for (ph, lo, hi) in reads if ph in PSUM_NAMES} | {ph for (ph, lo, hi) in writes if ph in PSUM_NAMES}
        for bk in banks:
            la = self.bank_last.setdefault(bk, {})
            for e2, o2 in la.items():
                if e2 != eng:
                    deps.add(o2)
            la[eng] = oid
        for (ph, lo, hi) in writes:
            lst = self.recs.setdefault(ph, [])
            lst[:] = [r for r in lst if not (lo <= r[0] and r[1] <= hi)]
            lst.append((lo, hi, oid, True, eng))
        for (ph, lo, hi) in reads:
            lst = self.recs.setdefault(ph, [])
            if dma is None:
                lst[:] = [r for r in lst if not ((not r[3]) and r[4] == eng and r[0] == lo and r[1] == hi)]
            lst.append((lo, hi, oid, False, eng))
        waits = {}
        for d in deps:
            od = self.ops[d]
            if od["dma"] is not None:
                k = ("dma", od["dma"])
                waits[k] = self.sem_total[od["dma"]]
            else:
                if od["eng"] == "pe" and eng == "pe":
                    continue
                od["signal"] = True
                k = ("eng", od["eng"])
                waits[k] = max(waits.get(k, -1), d)
        if dma is not None:
            self.sem_total[dma] = self.sem_total.get(dma, 0) + 16
        self.ops.append(dict(eng=eng, fn=fn, waits=waits, dma=dma, signal=False))
        if is_out:
            self.final_deps.append(oid)
        return oid

    def finalize(self):
        waits = {}
        for d in self.final_deps:
            od = self.ops[d]
            waits[("dma", od["dma"])] = self.sem_total[od["dma"]]
        self.ops.append(dict(eng="sp", fn=None, waits=waits, dma=None, signal=False))


def build_program(dbg_point=None):
    nc = bass.Bass("TRN2", target_bir_lowering=False)

    def din(name, shape):
        return nc.dram_tensor(name, list(shape), F32, kind="ExternalInput").ap()

    def dout(name, shape):
        return nc.dram_tensor(name, list(shape), F32, kind="ExternalOutput").ap()

    xin = din("xin", [TCORE, D])
    wgu = din("wgu", [4, NJ, 128, 2048])
    wd = din("wd", [4, 8, 2, 128, 1408])
    wmi = din("wmi", [23, 128, 1024])
    wmo = din("wmo", [8, 128, 1024])
    wpl = din("wpl", [128, 2048])
    cosd = din("cosd", [128, TCORE])
    sind = din("sind", [128, TCORE])
    cstd = din("cst", [128, NCST])
    ckd = din("ck", [16, 128, 128])
    cvd = din("cv", [16, 128, 128])
    sconvd = din("sconv", [16, 2, 512])
    spoold = din("spool", [16, 15, 1024])
    y_main = dout("y_main", [2048, D])
    y_samp = dout("y_samp", [64, D])
    kpd = dout("kp", [128, 128])
    vpd = dout("vp", [128, 128])
    convpd = dout("convp", [2, 512])
    poolpd = dout("poolp", [15, 1024])
    ksd = dout("ks", [16, 128, 128])
    vsd = dout("vs", [16, 128, 128])
    convsd = dout("convs", [16, 2, 512])
    poolsd = dout("pools", [16, 15, 1024])
    dbgd = dout("dbg", [128, 8 * TG]) if dbg_point is not None else None

    es = ExitStack()
    with es:
        def sb(name, shape, dt):
            return es.enter_context(nc.sbuf_tensor("sb_" + name, list(shape), dt))

        def ps(name, shape, dt):
            return es.enter_context(nc.psum_tensor("ps_" + name, list(shape), dt))

        def mk(t, name, fshape, dt):
            esz = 4 if dt == F32 else 2
            return LT(name, t[:] if hasattr(t, "__getitem__") else t, 0, esz, fshape)

        xT_t = sb("xT", [128, KC, TG], F32); xT = mk(xT_t, "xT", (KC, TG), F32)
        hT_t = sb("hT", [128, KC, TG], BF16); hT = mk(hT_t, "hT", (KC, TG), BF16)
        NOVL = 29696
        ovl_t = sb("ovl", [128, NOVL], BF16)

        def ov(byte_off, fshape, dt):
            esz = 4 if dt == F32 else 2
            n = 1
            for s in fshape:
                n *= s
            assert byte_off % 4 == 0 and byte_off + n * esz <= NOVL * 2, (byte_off, fshape)
            ap = ovl_t[:, byte_off // 2: byte_off // 2 + n * esz // 2]
            if dt == F32:
                ap = ap.bitcast(F32)
            if len(fshape) == 2:
                ap = ap.rearrange("p (a b) -> p a b", a=fshape[0])
            elif len(fshape) == 3:
                ap = ap.rearrange("p (a b c) -> p a b c", a=fshape[0], b=fshape[1])
            return LT("ovl", ap, byte_off, esz, fshape)

        lstg = ov(0, (8, 1024), F32)
        aT = ov(0, (NJ, TG), BF16)
        uT = ov(0, (4, TG + 2), F32)
        qT = ov(16416, (4, TG), BF16)
        attT = ov(24608, (4, TG), BF16)
        gcv = ov(32800, (4, TG), BF16)
        qpad = ov(40992, (16, 128), BF16)
        kctok = ov(45088, (16, 128), BF16)
        Vs = ov(49184, (16, 128), BF16)
        kTs = ov(53280, (16, 128), BF16)
        h1bf = ov(0, (8, 528), BF16)
        ppT = ov(8448, (8, 512), BF16)
        hsx = ov(45088, (8, 304), F32)
        pa = [ov(26368 + 1216 * i, (304,), F32) for i in range(4)]
        Dg = ov(31232, (4, 2, 128), BF16)
        wr_t = sb("wring", [128, NSLOT, 2048], BF16); wring = mk(wr_t, "wring", (NSLOT, 2048), BF16)
        kT_t = sb("kT", [128, 128 + TG], BF16); kT = mk(kT_t, "kT", (128 + TG,), BF16)
        vT_t = sb("vT", [128, TG], BF16); vT = mk(vT_t, "vT", (TG,), BF16)
        Vtok_t = sb("Vtok", [128, 9, 128], BF16); Vtok = mk(Vtok_t, "Vtok", (9, 128), BF16)
        kv32_t = sb("kv32", [128, 2, 192], F32); kv32 = mk(kv32_t, "kv32", (2, 192), F32)
        cos_t = sb("cos", [128, TG], F32); cosT = mk(cos_t, "cos", (TG,), F32)
        sin_t = sb("sin", [128, TG], F32); sinT = mk(sin_t, "sin", (TG,), F32)
        rstd_t = sb("rstd", [128, TG], F32); rstd = mk(rstd_t, "rstd", (TG,), F32)
        rtT_t = sb("rtT", [128, 8], F32); rtT = mk(rtT_t, "rtT", (8,), F32)
        rrT_t = sb("rrT", [128, 8], F32); rrT = mk(rrT_t, "rrT", (8,), F32)
        dmy_t = sb("dmy", [128, 4], F32); dmy = mk(dmy_t, "dmy", (4,), F32)
        ones32_t = sb("ones32", [128, 128], F32); ones32 = mk(ones32_t, "ones32", (128,), F32)
        sq_t = sb("sq", [128, 3, 512], BF16); sq = mk(sq_t, "sq", (3, 512), BF16)
        sg_t = sb("sg", [128, 2, 512], F32); sg = mk(sg_t, "sg", (2, 512), F32)
        rp_t = sb("rp", [128, 4, 512], F32); rp = mk(rp_t, "rp", (4, 512), F32)
        s_t = sb("s_sc", [128, 2, 256], F32); s_sc = mk(s_t, "s_sc", (2, 256), F32)
        pe_t = sb("pexp", [128, 2, 256], F32); pexp = mk(pe_t, "pexp", (2, 256), F32)
        pn_t = sb("pn", [128, 2, 256], BF16); pn = mk(pn_t, "pn", (2, 256), BF16)
        pT_t = sb("pT", [128, 2, 256], BF16); pT = mk(pT_t, "pT", (2, 256), BF16)
        st_t = sb("stat", [128, 2, 8], F32); stat = mk(st_t, "stat", (2, 8), F32)
        stg_t = sb("stg", [128, 2, 1024], F32); stg = mk(stg_t, "stg", (2, 1024), F32)
        cst_t = sb("cst", [128, NCST], F32); cst = mk(cst_t, "cst", (NCST,), F32)
        idb_t = sb("identb", [128, 128], BF16); identb = mk(idb_t, "identb", (128,), BF16)
        one_t = sb("onesD", [128, 128], BF16); onesD = mk(one_t, "onesD", (128,), BF16)
        h1c_t = sb("h1c", [128, 8, 15], F32); h1c = mk(h1c_t, "h1c", (8, 15), F32)
        h1b_t = sb("h1b", [128, 8, 15], F32); h1b = mk(h1b_t, "h1b", (8, 15), F32)
        ucar_t = sb("ucar", [128, 4, 2], F32); ucar = mk(ucar_t, "ucar", (4, 2), F32)
        usx_t = sb("usx", [128, 4, 96], F32); usx = mk(usx_t, "usx", (4, 96), F32)
        knc_t = sb("knc", [128, 2, 32], BF16); knc = mk(knc_t, "knc", (2, 32), BF16)
        vnc_t = sb("vnc", [128, 2, 32], BF16); vnc = mk(vnc_t, "vnc", (2, 32), BF16)
        vno_t = sb("vno", [128, 2, 128], BF16); vno = mk(vno_t, "vno", (2, 128), BF16)
        gu = []
        gu2b = []
        for i in range(2):
            t = ps(f"gupair{i}", [128, 1024], F32)
            gu2b.append(t)
            gu.append(LT(f"gu{2 * i}", t[:, 0:512], 0, 4, (512,)))
            gu.append(LT(f"gu{2 * i + 1}", t[:, 512:1024], 0, 4, (512,)))
        yb = []
        for i in range(2):
            t = ps(f"yb{i}", [128, 512], F32)
            yb.append(mk(t, f"yb{i}", (512,), F32))
        ptb = [LT(f"gu{i}", gu[i].ap.bitcast(BF16), 0, 2, (1024,)) for i in (2, 3)]
        tp_t = ps("tp", [128, 512], F32); tp = mk(tp_t, "tp", (512,), F32)
        tpb_t = ps("tpb", [128, 1024], BF16); tpb = mk(tpb_t, "tpb", (1024,), BF16)
        ptx = [LT("tpb", tpb_t[:].rearrange("p (a b) -> p a b", a=8), 0, 2, (8, 128)),
               LT("tp", tp_t[:].bitcast(BF16).rearrange("p (a b) -> p a b", a=8), 0, 2, (8, 128))]
        pb4_t = sb("pb4", [128, 2, 4, 256], F32); pb4 = mk(pb4_t, "pb4", (2, 4, 256), F32)
        pn4_t = sb("pn4", [128, 2, 4, 256], BF16); pn4 = mk(pn4_t, "pn4", (2, 4, 256), BF16)
        pT4_t = sb("pT4", [128, 2, 8, 128], BF16); pT4 = mk(pT4_t, "pT4", (2, 8, 128), BF16)
        st4_t = sb("st4", [128, 2, 16], F32); st4 = mk(st4_t, "st4", (2, 16), F32)
        maskb_t = sb("maskb", [128, 3, 256], BF16); maskb = mk(maskb_t, "maskb", (3, 256), BF16)
        nsm_t = sb("nsm", [128, 2], F32); nsm = mk(nsm_t, "nsm", (2,), F32)

        def cs(off, n=1):
            return cst(slice(off, off + n))

        ident = LT("cst", cst_t[:, C_ID:C_ID + 128], C_ID * 4, 4, (128,))

        def emit(S, wspecs):
            cnt = dict(gu=0, yb=0, io=0, sq=0, sg=0, att=0, cp=0, w=0, wiss=0, tb=0, fo=0, whold=None, pb=0)

            def OP(eng, f, reads, writes, **kw):
                rr = []
                for a in reads:
                    rr += a.r if isinstance(a, Acc) else a
                ww = []
                for a in writes:
                    ww += a.r if isinstance(a, Acc) else a
                return S.op(eng, f, rr, ww, **kw)

            def wissue(t):
                spec = wspecs[t]
                slot = t % NSLOT
                ncols = spec[1]
                dst = wring(slot, slice(0, ncols))
                src = spec[0]
                OP("pool", lambda e: e.dma_start(out=dst.ap, in_=src), [], [wring(slot)], dma=f"w{slot}")

            def wnext(src_ap, ncols, fshape):
                i = cnt["w"]
                cnt["w"] += 1
                if S.dry:
                    wspecs.append((src_ap, ncols))
                    slot = i % NSLOT
                else:
                    lim = min(i + NSLOT - 1, len(wspecs))
                    if cnt["whold"] is not None:
                        lim = min(lim, cnt["whold"] + NSLOT)
                    while cnt["wiss"] < lim:
                        wissue(cnt["wiss"])
                        cnt["wiss"] += 1
                    slot = i % NSLOT
                ap = wr_t[:, slot, 0:ncols]
                if len(fshape) == 2:
                    ap = ap.rearrange("p (a b) -> p a b", a=fshape[0])
                else:
                    ap = ap.rearrange("p (a b c) -> p a b c", a=fshape[0], b=fshape[1])
                return LT("wring", ap, slot * 4096, 2, fshape)

            def mm(out, lhsT, rhs, start, stop):
                OP("pe", lambda e: e.matmul(out.ap, lhsT=lhsT.ap, rhs=rhs.ap, start=start, stop=stop),
                   [lhsT, rhs], [out])

            def tr(out, in_, idn):
                OP("pe", lambda e: e.transpose(out.ap, in_.ap, idn.ap), [in_, idn], [out])

            def copy(out, in_, eng=None):
                if eng is None:
                    eng = "act" if cnt["cp"] % 2 == 0 else "dve"
                    cnt["cp"] += 1
                if eng == "act":
                    OP("act", lambda e: e.copy(out=out.ap, in_=in_.ap), [in_], [out])
                else:
                    OP("dve", lambda e: e.tensor_copy(out=out.ap, in_=in_.ap), [in_], [out])

            def tt(out, a, b, op, eng="dve"):
                OP(eng, lambda e: e.tensor_tensor(out=out.ap, in0=a.ap, in1=b.ap, op=op), [a, b], [out])

            def stt(out, a, scalar, b, op0, op1):
                sc = scalar.ap if isinstance(scalar, Acc) else scalar
                rd = [a, b] + ([scalar] if isinstance(scalar, Acc) else [])
                OP("dve", lambda e: e.scalar_tensor_tensor(out=out.ap, in0=a.ap, scalar=sc, in1=b.ap, op0=op0, op1=op1),
                   rd, [out])

            def ts(out, a, s1, s2, op0, op1=None):
                s1a = s1.ap if isinstance(s1, Acc) else s1
                s2a = s2.ap if isinstance(s2, Acc) else s2
                rd = [a] + [x for x in (s1, s2) if isinstance(x, Acc)]
                if op1 is None:
                    OP("dve", lambda e: e.tensor_scalar(out=out.ap, in0=a.ap, scalar1=s1a, scalar2=None, op0=op0), rd, [out])
                else:
                    OP("dve", lambda e: e.tensor_scalar(out=out.ap, in0=a.ap, scalar1=s1a, scalar2=s2a, op0=op0, op1=op1), rd, [out])

            def act(out, in_, func, bias=None, scale=None, accum=None):
                kw = {}
                rd = [in_]
                wr = [out]
                if bias is not None:
                    kw["bias"] = bias.ap if isinstance(bias, Acc) else bias
                    if isinstance(bias, Acc):
                        rd.append(bias)
                if scale is not None:
                    kw["scale"] = scale.ap if isinstance(scale, Acc) else scale
                    if isinstance(scale, Acc):
                        rd.append(scale)
                if accum is not None:
                    kw["accum_out"] = accum.ap
                    wr.append(accum)
                OP("act", lambda e: e.activation(out=out.ap, in_=in_.ap, func=func, **kw), rd, wr)

            def dma(eng, out_ap, in_ap, reads, writes, key, is_out=False):
                OP(eng, lambda e: e.dma_start(out=out_ap, in_=in_ap), reads, writes, dma=key, is_out=is_out)

            def gu_pair():
                p = cnt["gu"] % 2
                cnt["gu"] += 1
                return gu[2 * p], gu[2 * p + 1]

            def gu_one():
                p = cnt["gu"] % 2
                cnt["gu"] += 1
                return gu[2 * p]

            def y_bank():
                p = cnt["yb"] % 2
                cnt["yb"] += 1
                return yb[p]

            def io_slot():
                p = cnt["io"] % 2
                cnt["io"] += 1
                return p

            dma("sp", cst_t[:], cstd[:], [], [cst.whole()], "cst")
            OP("dve", lambda e: e.memset(one_t[:], 1.0 / 1024.0), [], [onesD.whole()])
            OP("dve", lambda e: e.memset(ones32_t[:], 1.0), [], [ones32.whole()])
            OP("dve", lambda e: e.memset(dmy_t[:], 1.0), [], [dmy.whole()])
            copy(identb(), ident(), eng="dve")
            src_m = LT("cst", cst_t[:, C_MASK:C_MASK + 768].rearrange("p (a b) -> p a b", a=3), C_MASK * 4, 4, (3, 256))
            copy(maskb(), src_m(), eng="dve")
            for kvh_ in range(2):
                OP("dve", lambda e, k_=kvh_: e.tensor_reduce(out=nsm_t[:, k_:k_ + 1], in_=cst_t[:, C_SINK + 4 * k_:C_SINK + 4 * k_ + 4], axis=AX.X, op=ALU.max),
                   [cst(slice(C_SINK + 4 * kvh_, C_SINK + 4 * kvh_ + 4))], [nsm(slice(kvh_, kvh_ + 1))])
            ts(nsm(), nsm(), -1.0, None, ALU.mult)

            def tbank():
                bl = [tp, yb[0], yb[1], gu[0], gu[1], gu[2], gu[3]]
                b_ = bl[cnt["tb"] % len(bl)]
                cnt["tb"] += 1
                return b_

            def load_x(g):
                for bi_, c0 in enumerate(range(0, g.n, 128)):
                    n = min(128, g.n - c0)
                    sl = bi_ % 8
                    dma("sp", lstg.ap[0:n, sl, :], xin[g.lo + c0: g.lo + c0 + n, :], [], [lstg(sl)], f"ld{sl}")
                    for half in range(2):
                        bk = tbank()
                        for kk in range(4):
                            k = half * 4 + kk
                            tr(bk(slice(kk * 128, kk * 128 + n)), lstg(sl, slice(k * 128, (k + 1) * 128), p=(0, n)),
                               LT("cst", cst_t[0:n, C_ID:C_ID + n], C_ID * 4, 4, (n,))())
                        src = LT(bk.phys, bk.ap.rearrange("p (a b) -> p a b", a=4), 0, 4, (4, 128))
                        copy(xT(slice(half * 4, half * 4 + 4), slice(c0, c0 + n)), src(slice(0, 4), slice(0, n)))

            sq8 = LT("rp", rp_t[:].rearrange("p a b -> p (a b)").bitcast(BF16).rearrange("p (a b) -> p a b", a=8), 0, 2, (8, 512))

            def norm_squares(c0, c1, nb):
                n = c1 - c0
                for k in range(KC):
                    act(sq8(k, slice(0, n)), xT(k, slice(c0, c1)), AF.Square)

            dgv = LT("s_sc", s_t[:].rearrange("p a b -> p (a b)").rearrange("p (a b) -> p a b", a=4), 0, 4, (4, 128))

            def preload(func, slot):
                act(dmy(slice(slot + 1, slot + 2)), dmy(slice(0, 1)), func)

            def norm_stats(c0, c1, nb, col0):
                n = c1 - c0
                nblk = (n + 127) // 128
                for b_ in range(nblk):
                    r0 = b_ * 128
                    nr = min(128, n - r0)
                    for k in range(KC):
                        mm(nb(slice(col0 + b_, col0 + b_ + 1), p=(0, nr)), sq8(k, slice(r0, r0 + nr)), onesD(slice(0, 1)), k == 0, k == KC - 1)
                return [min(128, n - b_ * 128) for b_ in range(nblk)]

            def norm_rsqrt(nb, ncols, rows=None):
                if rows is None:
                    rows = [128] * ncols
                c = 0
                while c < ncols:
                    e_ = c
                    while e_ < ncols and rows[e_] == rows[c]:
                        e_ += 1
                    pr = (0, rows[c])
                    act(rtT(slice(c, e_), p=pr), nb(slice(c, e_), p=pr), AF.Sqrt, bias=EPS, scale=1.0)
                    OP("dve", lambda e, o=rrT(slice(c, e_), p=pr), i=rtT(slice(c, e_), p=pr): e.reciprocal(out=o.ap, in_=i.ap),
                       [rtT(slice(c, e_), p=pr)], [rrT(slice(c, e_), p=pr)])
                    c = e_

            def norm_apply(c0, c1, gi, col0, write_h=True):
                n = c1 - c0
                nblk = (n + 127) // 128
                rb = y_bank()
                for b_ in range(nblk):
                    r0 = b_ * 128
                    nr = min(128, n - r0)
                    idn = LT("cst", cst_t[0:nr, C_ID:C_ID + nr], C_ID * 4, 4, (nr,))()
                    ts(dgv(b_, slice(0, nr), p=(0, nr)), idn, rrT(slice(col0 + b_, col0 + b_ + 1), p=(0, nr)), None, ALU.mult)
                    mm(rb(slice(r0, r0 + nr)), ones32(p=(0, nr)), dgv(b_, slice(0, nr), p=(0, nr)), True, True)
                if write_h:
                    for k in range(KC):
                        stt(hT(k, slice(c0, c1)), xT(k, slice(c0, c1)), cs(C_GAIN + gi * 8 + k),
                            rb(slice(0, n)), ALU.mult, ALU.mult)
                else:
                    copy(rstd(slice(c0, c1)), rb(slice(0, n)), eng="act")

            def norm(g, lo, hi, gi, write_h=True, flat=False, after=None):
                subs_ = subtiles_flat(lo, hi) if flat else subtiles(g, lo, hi)
                nb = y_bank()
                cols = []
                col = 0
                rows = []
                for (c0, c1, typ) in subs_:
                    norm_squares(c0, c1, nb)
                    cols.append(col)
                    rows += norm_stats(c0, c1, nb, col)
                    col = len(rows)
                norm_rsqrt(nb, col, rows)
                if after is not None:
                    preload(after, 1)
                for (c0, c1, typ), cl in zip(subs_, cols):
                    norm_apply(c0, c1, gi, cl, write_h)

            def ffn(g, lo, hi, f, gi):
                subs = subtiles_flat(lo, hi)

                def gu_step(j, w, c0, c1):
                    n = c1 - c0
                    gb, ub = gu_pair()
                    for k in range(KC):
                        mm(gb(slice(0, n)), w(0, k), hT(k, slice(c0, c1)), k == 0, k == KC - 1)
                    for k in range(KC):
                        mm(ub(slice(0, n)), w(1, k), hT(k, slice(c0, c1)), k == 0, k == KC - 1)
                    sl = cnt["sg"] % 2
                    cnt["sg"] += 1
                    act(sg(sl, slice(0, n)), gb(slice(0, n)), AF.Silu)
                    tt(aT(j, slice(c0, c1)), sg(sl, slice(0, n)), ub(slice(0, n)), ALU.mult)

                JH = 4 if len(subs) > 1 else 0
                ws = {}
                nb0 = y_bank()
                cols = []
                col = 0
                rows = []
                for (c0, c1, typ) in subs:
                    norm_squares(c0, c1, nb0)
                    cols.append(col)
                    rows += norm_stats(c0, c1, nb0, col)
                    col = len(rows)
                norm_rsqrt(nb0, col, rows)
                preload(AF.Silu, 1)
                norm_apply(subs[0][0], subs[0][1], gi, cols[0])

                def gu_pair_step(ja, jb, wa, wb, c0, c1):
                    n = c1 - c0
                    gA, uA = gu_pair()
                    gB, uB = gu_pair()
                    for k in range(KC):
                        mm(gA(slice(0, n)), wa(0, k), hT(k, slice(c0, c1)), k == 0, k == KC - 1)
                        mm(uA(slice(0, n)), wa(1, k), hT(k, slice(c0, c1)), k == 0, k == KC - 1)
                        mm(gB(slice(0, n)), wb(0, k), hT(k, slice(c0, c1)), k == 0, k == KC - 1)
                        mm(uB(slice(0, n)), wb(1, k), hT(k, slice(c0, c1)), k == 0, k == KC - 1)
                    for (j_, g_, u_) in ((ja, gA, uA), (jb, gB, uB)):
                        sl = cnt["sg"] % 2
                        cnt["sg"] += 1
                        act(sg(sl, slice(0, n)), g_(slice(0, n)), AF.Silu)
                        tt(aT(j_, slice(c0, c1)), sg(sl, slice(0, n)), u_(slice(0, n)), ALU.mult)

                cnt["whold"] = cnt["w"]
                ws[0] = wnext(wgu[f, 0], 2048, (2, KC, 128))
                ws[1] = wnext(wgu[f, 1], 2048, (2, KC, 128))
                gu_pair_step(0, 1, ws[0], ws[1], subs[0][0], subs[0][1])
                JS = 2
                if JH:
                    assert len(subs) == 2
                    norm_apply(subs[1][0], subs[1][1], gi, cols[1])
                    for j in range(2, JH):
                        ws[j] = wnext(wgu[f, j], 2048, (2, KC, 128))
                        gu_step(j, ws[j], subs[0][0], subs[0][1])
                    for j in range(JH):
                        gu_step(j, ws[j], subs[1][0], subs[1][1])
                    JS = JH
                cnt["whold"] = None
                for j in range(JS, NJ):
                    w = wnext(wgu[f, j], 2048, (2, KC, 128))
                    for (c0, c1, typ) in subs:
                        gu_step(j, w, c0, c1)
                preload(AF.Sqrt, 0)
                for m in range(KC):
                    w0 = wnext(wd[f, m, 0], 1408, (11, 128))
                    w1 = wnext(wd[f, m, 1], 1408, (11, 128))
                    for (c0, c1, typ) in subs:
                        n = c1 - c0
                        y = y_bank()
                        for hf, w in ((0, w0), (1, w1)):
                            for jj in range(11):
                                mm(y(slice(0, n)), w(jj), aT(hf * 11 + jj, slice(c0, c1)),
                                   hf == 0 and jj == 0, hf == 1 and jj == 10)
                        stt(xT(m, slice(c0, c1)), y(slice(0, n)), 0.5, xT(m, slice(c0, c1)), ALU.mult, ALU.add)

            def proj(w, c0, c1, bank):
                n = c1 - c0
                for k in range(KC):
                    mm(bank(slice(0, n)), w(k), hT(k, slice(c0, c1)), k == 0, k == KC - 1)

            def rope(A_, B_, c0, c1, dst):
                n = c1 - c0
                tt(rp(0, slice(0, n)), A_(slice(0, n)), cosT(slice(c0, c1)), ALU.mult)
                tt(rp(1, slice(0, n)), B_(slice(0, n)), sinT(slice(c0, c1)), ALU.mult)
                tt(dst, rp(0, slice(0, n)), rp(1, slice(0, n)), ALU.add)

            def keep32(g, which, src_tmp, c0, c1):
                rngs = []
                if g is GROUPS[2]:
                    rngs.append((128, 256, 0))
                    rngs.append((256, 320, 128))
                for (lo, hi, off) in rngs:
                    a, b = max(lo, c0), min(hi, c1)
                    if a < b:
                        copy(kv32(which, slice(off + a - lo, off + b - lo)), src_tmp(slice(a - c0, b - c0)), eng="dve")

            def mixer(g):
                dma("sp", cos_t[:, 0:g.n], cosd[:, g.lo:g.lo + g.n], [], [cosT(slice(0, g.n))], "cos")
                dma("sp", sin_t[:, 0:g.n], sind[:, g.lo:g.lo + g.n], [], [sinT(slice(0, g.n))], "sin")
                norm(g, 0, g.n, 1, after=AF.Exp)
                subs = subtiles(g, 0, g.n)
                for hp in range(4):
                    wq = wnext(wmi[hp], 1024, (KC, 128))
                    wp = wnext(wmi[4 + hp], 1024, (KC, 128))
                    for (c0, c1, typ) in subs:
                        A_, B_ = gu_pair()
                        proj(wq, c0, c1, A_)
                        proj(wp, c0, c1, B_)
                        rope(A_, B_, c0, c1, qT(hp, slice(c0, c1)))
                wk = wnext(wmi[8], 1024, (KC, 128))
                wkp = wnext(wmi[9], 1024, (KC, 128))
                for (c0, c1, typ) in subs:
                    n = c1 - c0
                    A_, B_ = gu_pair()
                    proj(wk, c0, c1, A_)
                    proj(wkp, c0, c1, B_)
                    rope(A_, B_, c0, c1, rp(2, slice(0, n)))
                    copy(kT(slice(128 + c0, 128 + c1)), rp(2, slice(0, n)), eng="act")
                    keep32(g, 0, lambda sl_: rp(2, sl_), c0, c1)
                wv = wnext(wmi[10], 1024, (KC, 128))
                for (c0, c1, typ) in subs:
                    n = c1 - c0
                    A_ = gu_one()
                    proj(wv, c0, c1, A_)
                    copy(rp(3, slice(0, n)), A_(slice(0, n)), eng="act")
                    copy(vT(slice(c0, c1)), rp(3, slice(0, n)), eng="dve")
                    keep32(g, 1, lambda sl_: rp(3, sl_), c0, c1)
                for i in range(4):
                    wc = wnext(wmi[11 + i], 1024, (KC, 128))
                    wh = wnext(wmi[15 + i], 1024, (KC, 128))
                    for (c0, c1, typ) in subs:
                        n = c1 - c0
                        A_, B_ = gu_pair()
                        proj(wc, c0, c1, A_)
                        proj(wh, c0, c1, B_)
                        copy(rp(0, slice(0, n)), A_(slice(0, n)), eng="act")
                        dst = uT(i, slice(2 + c0, 2 + c1)) if typ == 'p' else usx(i, slice(32, 96))
                        tt(dst, rp(0, slice(0, n)), B_(slice(0, n)), ALU.mult)
                if g is not GROUPS[0]:
                    copy(uT(slice(0, 4), slice(0, 2)), ucar(), eng="dve")
                dbg_dump(f"g{GROUPS.index(g)}.proj")
                attention(g)
                dbg_dump(f"g{GROUPS.index(g)}.att")
                for i in range(4):
                    wb = wnext(wmi[19 + i], 1024, (KC, 128))
                    for (c0, c1, typ) in subtiles(g, g.noA, g.n):
                        n = c1 - c0
                        A_ = gu_one()
                        proj(wb, c0, c1, A_)
                        cvt = rp(1, slice(0, n))
                        if typ == 'p':
                            u0, u1, u2 = (uT(i, slice(c0 + d, c1 + d)) for d in range(3))
                        else:
                            u0, u1, u2 = (usx(i, slice(16 * d, 16 * d + 64)) for d in range(3))
                        ts(cvt, u0, cs(C_CONVW + 0 * 4 + i), None, ALU.mult)
                        stt(cvt, u1, cs(C_CONVW + 1 * 4 + i), cvt, ALU.mult, ALU.add)
                        stt(cvt, u2, cs(C_CONVW + 2 * 4 + i), cvt, ALU.mult, ALU.add)
                        tt(gcv(i, slice(c0, c1)), cvt, A_(slice(0, n)), ALU.mult)
                preload(AF.Sqrt, 0)
                for m in range(KC):
                    w = wnext(wmo[m], 1024, (KC, 128))
                    for (c0, c1, typ) in subtiles(g, g.noA, g.n):
                        n = c1 - c0
                        y = y_bank()
                        for kc in range(KC):
                            rhs = attT(kc, slice(c0, c1)) if kc < 4 else gcv(kc - 4, slice(c0, c1))
                            mm(y(slice(0, n)), w(kc), rhs, kc == 0, kc == KC - 1)
                        tt(xT(m, slice(c0, c1)), y(slice(0, n)), xT(m, slice(c0, c1)), ALU.add)
                dbg_dump(f"g{GROUPS.index(g)}.mo")
                if g is GROUPS[2]:
                    conv_outputs(g)
                else:
                    e = g.pend
                    copy(ucar(), uT(slice(0, 4), slice(2 + e - 2, 2 + e)), eng="dve")
                    copy(kT(slice(0, 128)), kT(slice(128 + e - 128, 128 + e)), eng="dve")
                    copy(Vtok(0), Vtok(e // 128), eng="dve")

            def softmax_pv_common(sbank, width, mask_acc, sink_acc, slot):
                s_ = s_sc(slot, slice(0, width))
                stt(s_, sbank(slice(0, width)), 0.125, mask_acc, ALU.mult, ALU.add)
                mx = stat(slot, slice(0, 1))
                negm = stat(slot, slice(1, 2))
                rsum = stat(slot, slice(2, 3))
                esk = stat(slot, slice(3, 4))
                den = stat(slot, slice(4, 5))
                rden = stat(slot, slice(5, 6))
                OP("dve", lambda e: e.tensor_reduce(out=mx.ap, in_=s_.ap, axis=AX.X, op=ALU.max), [s_], [mx])
                ts(negm, mx, sink_acc, -1.0, ALU.max, ALU.mult)
                pe_ = pexp(slot, slice(0, width))
                act(pe_, s_, AF.Exp, bias=negm, scale=1.0, accum=rsum)
                act(esk, sink_acc, AF.Exp, bias=negm, scale=1.0)
                tt(den, rsum, esk, ALU.add)
                OP("dve", lambda e: e.reciprocal(out=rden.ap, in_=den.ap), [den], [rden])
                act(pn(slot, slice(0, width)), pe_, AF.Copy, scale=rden)

            def attention(g):
                for c0 in range(0, g.pend, 128):
                    bi = c0 // 128 + 1
                    tr(tpb(slice(0, 128)), vT(slice(c0, c0 + 128)), identb())
                    copy(Vtok(bi), tpb(slice(0, 128)))
                start = g.B[0] if g.B else 0
                batches = []
                for c0 in range(start, g.pend, 128):
                    if g.B and c0 == g.B[0]:
                        mi = 1
                    elif g is GROUPS[0] and c0 == g.M[0]:
                        mi = 2
                    else:
                        mi = 0
                    for kvh in range(2):
                        batches.append((c0, kvh, mi))
                obs = {}

                def S_mm(i):
                    c0, kvh, mi = batches[i]
                    sl = i % 2
                    ph = (64 * kvh, 64 * kvh + 64)
                    for hp in range(4):
                        bank = gu[2 * sl + hp // 2]
                        reg = bank(slice((hp % 2) * 256, (hp % 2) * 256 + 256))
                        mm(reg, identb(), maskb(mi), True, False)
                        mm(reg, qT(hp, slice(c0, c0 + 128), p=ph), kT(slice(c0, c0 + 256), p=ph), False, True)

                def max_min(i):
                    c0, kvh, mi = batches[i]
                    sl = i % 2
                    Sall = Acc(gu2b[sl][:, :], gu[2 * sl]().r + gu[2 * sl + 1]().r)
                    mx = st4(sl, slice(0, 1))
                    OP("dve", lambda e: e.tensor_reduce(out=mx.ap, in_=Sall.ap, axis=AX.X, op=ALU.max), [Sall], [mx])
                    ts(st4(sl, slice(1, 2)), mx, -0.125, nsm(slice(kvh, kvh + 1)), ALU.mult, ALU.min)

                def exps(i):
                    c0, kvh, mi = batches[i]
                    sl = i % 2
                    negm = st4(sl, slice(1, 2))
                    for hp in range(4):
                        bank = gu[2 * sl + hp // 2]
                        act(pb4(sl, hp), bank(slice((hp % 2) * 256, (hp % 2) * 256 + 256)), AF.Exp, bias=negm, scale=0.125,
                            accum=st4(sl, slice(4 + hp, 5 + hp)))
                    act(st4(sl, slice(8, 12)), cst(slice(C_SINK + 4 * kvh, C_SINK + 4 * kvh + 4)), AF.Exp, bias=negm, scale=1.0)

                def den_pn(i):
                    sl = i % 2
                    den = st4(sl, slice(12, 16))
                    tt(den, st4(sl, slice(4, 8)), st4(sl, slice(8, 12)), ALU.add)
                    OP("dve", lambda e: e.reciprocal(out=den.ap, in_=den.ap), [den], [den])
                    for hp in range(4):
                        o_, i_, sc_ = pn4(sl, hp), pb4(sl, hp), st4(sl, slice(12 + hp, 13 + hp))
                        if hp % 2 == 0:
                            act(o_, i_, AF.Copy, scale=sc_)
                        else:
                            ts(o_, i_, sc_, None, ALU.mult)

                def trs(i):
                    sl = i % 2
                    px = ptx[sl]
                    for hp in range(4):
                        for hf in range(2):
                            tr(px(2 * hp + hf), pn4(sl, hp, slice(hf * 128, hf * 128 + 128)), identb())

                def pT_copy(i):
                    sl = i % 2
                    copy(pT4(sl), ptx[sl](), eng="dve")

                def PV(i):
                    c0, kvh, mi = batches[i]
                    sl = i % 2
                    bi = c0 // 128 + 1
                    ph = (64 * kvh, 64 * kvh + 64)
                    if kvh == 0:
                        obs[c0] = y_bank()
                    ob = obs[c0]
                    for hp in range(4):
                        oreg = ob(slice(hp * 128, hp * 128 + 128), p=ph)
                        mm(oreg, Vtok(bi - 1, slice(kvh * 64, kvh * 64 + 64)), pT4(sl, 2 * hp), True, False)
                        mm(oreg, Vtok(bi, slice(kvh * 64, kvh * 64 + 64)), pT4(sl, 2 * hp + 1), False, True)

                def att_copy(i):
                    c0, kvh, mi = batches[i]
                    if kvh == 1:
                        ob = obs[c0]
                        src = LT(ob.phys, ob.ap.rearrange("p (a b) -> p a b", a=4), 0, 4, (4, 128))
                        copy(attT(slice(0, 4), slice(c0, c0 + 128)), src(), eng="act")

                nb = len(batches)
                for it in range(nb + 2):
                    a, b_, c_ = it, it - 1, it - 2
                    if 0 <= c_ < nb:
                        trs(c_)
                        pT_copy(c_)
                    if a < nb:
                        S_mm(a)
                    if 0 <= c_ < nb:
                        PV(c_)
                    if 0 <= b_ < nb:
                        exps(b_)
                    if 0 <= b_ < nb:
                        den_pn(b_)
                    if a < nb:
                        max_min(a)
                    if 0 <= c_ < nb:
                        att_copy(c_)
                dbg_dump(f"g{GROUPS.index(g)}.attp")
                if g.S:
                    sample_attention(g)
                dbg_dump(f"g{GROUPS.index(g)}.atts")
                if g is GROUPS[2]:
                    kv_outputs(g)

            def sample_attention(g):
                S0 = g.S[0]
                ksrc = kT_t[:, 128 + S0:128 + S0 + 64].rearrange("p (t o i) -> p o t i", t=4, o=2, i=8)
                vsrc = vT_t[:, S0:S0 + 64].rearrange("p (t o i) -> p o t i", t=4, o=2, i=8)
                kdst = knc_t[:].rearrange("p o (t i) -> p o t i", t=4)
                vdst = vnc_t[:].rearrange("p o (t i) -> p o t i", t=4)
                OP("dve", lambda e: e.tensor_copy(out=kdst, in_=ksrc), [kT(slice(128 + S0, 128 + S0 + 64))], [knc.whole()])
                OP("dve", lambda e: e.tensor_copy(out=vdst, in_=vsrc), [vT(slice(S0, S0 + 64))], [vnc.whole()])
                OP("dve", lambda e: e.memset(vno_t[:], 0.0), [], [vno.whole()])
                for o in range(2):
                    tr(tpb(slice(0, 128), p=(0, 32)), vnc(o), identb())
                    copy(vno(o, p=(0, 32)), tpb(slice(0, 128), p=(0, 32)))
                dbg_dump("g2.sa3")
                OP("dve", lambda e: e.memset(ovl_t[:, 40992 // 2: 40992 // 2 + 2048], 0.0), [], [qpad.whole()])
                for gq in range(4):
                    for i in range(8):
                        srcap = qT.ap[:, gq, S0:S0 + 64].rearrange("p (t o i) -> p i o t", t=4, o=2, i=8)[:, i]
                        dstap = qpad.ap.rearrange("p (o i) c -> p i o c", o=2)[:, i, :, 16 * i + 4 * gq:16 * i + 4 * gq + 4]
                        rd = [qT(gq, slice(S0, S0 + 64))]
                        wrr = [qpad(i, slice(16 * i + 4 * gq, 16 * i + 4 * gq + 4)),
                               qpad(8 + i, slice(16 * i + 4 * gq, 16 * i + 4 * gq + 4))]
                        if (gq * 8 + i) % 2 == 0:
                            OP("dve", lambda e, d=dstap, s=srcap: e.tensor_copy(out=d, in_=s), rd, wrr)
                        else:
                            OP("act", lambda e, d=dstap, s=srcap: e.copy(out=d, in_=s), rd, wrr)
                dbg_dump("g2.sa4")
                mask = cst(slice(C_MASKS, C_MASKS + 160))
                for kvh in range(2):
                    ph = (64 * kvh, 64 * kvh + 64)
                    for o in range(2):
                        slot = cnt["att"] % 2
                        sbk = gu[slot]
                        pb = ptb[slot]
                        cnt["att"] += 1
                        for i in range(8):
                            mm(sbk(slice(0, 128)), qpad(8 * o + i, p=ph), kTs(8 * o + i, p=ph), i == 0, i == 7)
                        for i in range(8):
                            mm(sbk(slice(128, 160)), qpad(8 * o + i, p=ph), knc(o, p=ph), i == 0, i == 7)
                        softmax_pv_common(sbk, 160, mask, cs(C_SINKS + kvh), slot)
                        dbg_dump("g2.sa5")
                        tr(pb(slice(0, 128)), pn(slot, slice(0, 128)), identb())
                        tr(pb(slice(128, 256), p=(0, 32)), pn(slot, slice(128, 160)), identb())
                        copy(pT(slot, slice(0, 128)), pb(slice(0, 128)))
                        OP("dve", lambda e, sl_=slot: e.memset(pT_t[:, sl_, 128:256], 0.0), [], [pT(slot, slice(128, 256))])
                        copy(pT(slot, slice(128, 256), p=(0, 32)), pb(slice(128, 256), p=(0, 32)))
                        dbg_dump("g2.sa6")
                        ob = y_bank()
                        for i in range(8):
                            oreg = ob(slice(16 * i, 16 * i + 16), p=ph)
                            mm(oreg, Vs(8 * o + i, slice(kvh * 64, kvh * 64 + 64)), pT(slot, slice(16 * i, 16 * i + 16)), True, False)
                            mm(oreg, vno(o, slice(kvh * 64, kvh * 64 + 64)),
                               pT(slot, slice(128 + 16 * i, 128 + 16 * i + 16)), False, True)
                        dbg_dump("g2.sa7")
                        dstap = attT.ap[ph[0]:ph[1], :, S0:S0 + 64].rearrange("p g (t o i) -> p o i g t", t=4, o=2, i=8)[:, o]
                        srcap = ob.ap[ph[0]:ph[1], 0:128].rearrange("p (i g t) -> p i g t", i=8, g=4, t=4)
                        OP("dve", lambda e, d=dstap, s=srcap: e.tensor_copy(out=d, in_=s),
                           [ob(slice(0, 128))], [attT(slice(0, 4), slice(S0, S0 + 64))])

            def kv_outputs(g):
                sl = io_slot()
                for w_ in range(2):
                    tr(tp(slice(w_ * 128, w_ * 128 + 128)), kv32(w_, slice(0, 128)), ident())
                copy(stg(sl, slice(0, 256)), tp(slice(0, 256)))
                dma("sp", kpd[:, :], stg_t[:, sl, 0:128], [stg(sl)], [], f"io{sl}", is_out=True)
                dma("sp", vpd[:, :], stg_t[:, sl, 128:256], [stg(sl)], [], f"io{sl}", is_out=True)
                sl = io_slot()
                for w_ in range(2):
                    tr(tp(slice(256 + w_ * 128, 256 + w_ * 128 + 128), p=(0, 64)), kv32(w_, slice(128, 192)), ident())
                copy(stg(sl, slice(0, 256), p=(0, 64)), tp(slice(256, 512), p=(0, 64)))
                for t in range(4):
                    dma("sp", ksd[:, 124 + t, :], stg_t[t * 16:t * 16 + 16, sl, 0:128], [stg(sl)], [], f"io{sl}", is_out=True)
                    dma("sp", vsd[:, 124 + t, :], stg_t[t * 16:t * 16 + 16, sl, 128:256], [stg(sl)], [], f"io{sl}", is_out=True)

            def conv_hist_load(g):
                sl = io_slot()
                for r in range(2):
                    dma("sp", stg_t[r * 16:r * 16 + 16, sl, 0:512], sconvd[:, r, :], [], [stg(sl)], f"io{sl}")
                for i in range(4):
                    tr(tp(slice(i * 32, i * 32 + 32)), stg(sl, slice(i * 128, i * 128 + 128), p=(0, 32)),
                       LT("cst", cst_t[0:32, C_ID:C_ID + 32], C_ID * 4, 4, (32,))())
                src = LT("tp", tp_t[:, 0:128].rearrange("p (a b) -> p a b", a=4), 0, 4, (4, 32))
                copy(usx(slice(0, 4), slice(0, 32)), src())

            def conv_outputs(g):
                sl = io_slot()
                for i in range(4):
                    tr(tp(slice(i * 128, i * 128 + 128), p=(0, 32)), usx(i, slice(64, 96)), ident())
                copy(stg(sl, slice(0, 512), p=(0, 32)), tp(slice(0, 512), p=(0, 32)))
                for r in range(2):
                    dma("sp", convsd[:, r, :], stg_t[r * 16:r * 16 + 16, sl, 0:512], [stg(sl)], [], f"io{sl}", is_out=True)
                sl = io_slot()
                e = g.pend
                for i in range(4):
                    tr(tp(slice(i * 128, i * 128 + 128), p=(0, 2)), uT(i, slice(2 + e - 2, 2 + e)), ident())
                copy(stg(sl, slice(0, 512), p=(0, 2)), tp(slice(0, 512), p=(0, 2)))
                dma("sp", convpd[:, :], stg_t[0:2, sl, 0:512], [stg(sl)], [], f"io{sl}", is_out=True)

            hist_slots = []

            def pool_hist_dma(g):
                for part, (r0, r1) in enumerate(((0, 8), (8, 15))):
                    sl = io_slot()
                    hist_slots.append(sl)
                    for r in range(r0, r1):
                        dma("sp", stg_t[(r - r0) * 16:(r - r0) * 16 + 16, sl, :], spoold[:, r, :], [], [stg(sl)], f"io{sl}")

            def pool_hist_tr(g):
                for part, (r0, r1) in enumerate(((0, 8), (8, 15))):
                    sl = hist_slots[part]
                    nr = (r1 - r0) * 16
                    for half in range(2):
                        bk = tbank()
                        for kk in range(4):
                            k = half * 4 + kk
                            tr(bk(slice(kk * 128, kk * 128 + nr)), stg(sl, slice(k * 128, k * 128 + 128), p=(0, nr)),
                               LT("cst", cst_t[0:nr, C_ID:C_ID + nr], C_ID * 4, 4, (nr,))())
                        src = LT(bk.phys, bk.ap.rearrange("p (a b) -> p a b", a=4), 0, 4, (4, 128))
                        copy(hsx(slice(half * 4, half * 4 + 4), slice(r0 * 16, r0 * 16 + nr)), src(slice(0, 4), slice(0, nr)))

            def pool_stage(g):
                gi_n = 4
                norm(g, g.noA, g.n, gi_n, write_h=False)
                wp = wnext(wpl, 2048, (8, 2, 128))
                e = g.pend
                for k in range(KC):
                    stt(h1c(k), xT(k, slice(e - 15, e)), cs(C_GAIN + gi_n * 8 + k), rstd(slice(e - 15, e)), ALU.mult, ALU.mult)
                have_carry = g is not GROUPS[0]
                if g is GROUPS[2]:
                    sl = io_slot()
                    for half in range(2):
                        for kk in range(4):
                            k = half * 4 + kk
                            tr(tp(slice(kk * 128, kk * 128 + 128), p=(0, 15)), h1c(k), ident())
                        copy(stg(sl, slice(half * 512, half * 512 + 512), p=(0, 15)), tp(slice(0, 512), p=(0, 15)))
                    dma("sp", poolpd[:, :], stg_t[0:15, sl, :], [stg(sl)], [], f"io{sl}", is_out=True)
                psubs = subtiles(g, g.out, g.n)
                assert sum(1 for s_ in psubs if s_[2] == 'p') <= 2
                for si, (c0, c1, typ) in enumerate(psubs):
                    if typ == 'p' and si > 0:
                        for k in range(KC):
                            stt(h1b(k), xT(k, slice(c0 - 15, c0)), cs(C_GAIN + gi_n * 8 + k), rstd(slice(c0 - 15, c0)), ALU.mult, ALU.mult)
                for gi in range(4):
                    win = 2 << gi
                    ts(Dg(gi, 0), identb(), 1.0 / win - 1.0, None, ALU.mult)
                    ts(Dg(gi, 1), identb(), 1.0 / win, None, ALU.mult)
                for si, (c0, c1, typ) in enumerate(psubs):
                    n = c1 - c0
                    if typ == 'p':
                        W = 15 + n
                        for k in range(KC):
                            gk = cs(C_GAIN + gi_n * 8 + k)
                            if si == 0 and c0 >= 15:
                                stt(h1bf(k, slice(0, W)), xT(k, slice(c0 - 15, c1)), gk, rstd(slice(c0 - 15, c1)), ALU.mult, ALU.mult)
                            else:
                                if si == 0:
                                    assert c0 == 0
                                    copy(h1bf(k, slice(0, 15)), h1carry_prev(k), eng="act")
                                else:
                                    copy(h1bf(k, slice(0, 15)), h1b(k), eng="act")
                                stt(h1bf(k, slice(15, W)), xT(k, slice(c0, c1)), gk, rstd(slice(c0, c1)), ALU.mult, ALU.mult)

                        def p_stage(gi):
                            win = 2 << gi
                            for kc in range(2):
                                k = 2 * gi + kc
                                pbk = gu[cnt["pb"] % 4]
                                cnt["pb"] += 1
                                for i in range(win):
                                    mm(pbk(slice(0, n)), Dg(gi, 0 if i == 0 else 1), h1bf(k, slice(15 - i, 15 - i + n)), i == 0, i == win - 1)
                                copy(ppT(k, slice(0, n)), pbk(slice(0, n)), eng="act")

                        def z_stage(gi):
                            for e_ in range(2):
                                z = y_bank()
                                for kc in range(2):
                                    mm(z(slice(0, n)), wp(gi * 2 + e_, kc), ppT(2 * gi + kc, slice(0, n)), kc == 0, kc == 1)
                                m = 2 * gi + e_
                                stt(xT(m, slice(c0, c1)), z(slice(0, n)), cs(C_PSC + m), xT(m, slice(c0, c1)), ALU.mult, ALU.add)

                        for gi in range(5):
                            if gi < 4:
                                p_stage(gi)
                            if gi >= 1:
                                z_stage(gi - 1)
                        continue
                    step, W = 16, 19
                    for gi in range(4):
                        win = 2 << gi
                        for kc in range(2):
                            k = 2 * gi + kc
                            gk = cs(C_GAIN + gi_n * 8 + k)
                            stt(hsx(k, slice(240, 304)), xT(k, slice(c0, c1)), gk, rstd(slice(c0, c1)), ALU.mult, ALU.mult)
                            H = lambda a, b, k=k: hsx(k, slice(a * 16, b * 16))
                            P = lambda idx, a, b: pa[idx](slice(a * step, b * step))
                            if gi == 0:
                                tt(P(0, 14, W - 1), H(15, W), H(14, W - 1), ALU.add)
                                wsum = P(0, 14, W - 1)
                            else:
                                tt(P(0, 0, W - 1), H(1, W), H(0, W - 1), ALU.add)
                            if gi == 1:
                                tt(P(1, 12, W - 3), P(0, 14, W - 1), P(0, 12, W - 3), ALU.add)
                                wsum = P(1, 12, W - 3)
                            elif gi > 1:
                                tt(P(1, 0, W - 3), P(0, 2, W - 1), P(0, 0, W - 3), ALU.add)
                            if gi == 2:
                                tt(P(2, 8, W - 7), P(1, 12, W - 3), P(1, 8, W - 7), ALU.add)
                                wsum = P(2, 8, W - 7)
                            elif gi > 2:
                                tt(P(2, 0, W - 7), P(1, 4, W - 3), P(1, 0, W - 7), ALU.add)
                                tt(P(3, 0, W - 15), P(2, 8, W - 7), P(2, 0, W - 15), ALU.add)
                                wsum = P(3, 0, W - 15)
                            stt(ppT(k, slice(0, n)), wsum, 1.0 / win, H(15, W), ALU.mult, ALU.subtract)
                        for e_ in range(2):
                            z = y_bank()
                            for kc in range(2):
                                mm(z(slice(0, n)), wp(gi * 2 + e_, kc), ppT(2 * gi + kc, slice(0, n)), kc == 0, kc == 1)
                            m = 2 * gi + e_
                            stt(xT(m, slice(c0, c1)), z(slice(0, n)), cs(C_PSC + m), xT(m, slice(c0, c1)), ALU.mult, ALU.add)
                    pool_sample_out(g)

            h1prev_t = None

            def h1carry_prev(k):
                return h1p(k)

            def pool_sample_out(g):
                for part, (r0, r1) in enumerate(((4, 12), (12, 19))):
                    sl = io_slot()
                    nr = (r1 - r0) * 16
                    for half in range(2):
                        for kk in range(4):
                            k = half * 4 + kk
                            tr(tp(slice(kk * 128, kk * 128 + 128), p=(0, nr)), hsx(k, slice(r0 * 16, r1 * 16)), ident())
                        copy(stg(sl, slice(half * 512, half * 512 + 512), p=(0, nr)), tp(slice(0, 512), p=(0, nr)))
                    for r in range(r0, r1):
                        dma("sp", poolsd[:, r - 4, :], stg_t[(r - r0) * 16:(r - r0) * 16 + 16, sl, :], [stg(sl)], [], f"io{sl}", is_out=True)

            def final_out(g):
                norm(g, g.out, g.n, 6, write_h=False)
                for (c0, c1, typ) in subtiles(g, g.out, g.n):
                    for k in range(KC):
                        stt(xT(k, slice(c0, c1)), xT(k, slice(c0, c1)), cs(C_GAIN + 6 * 8 + k), rstd(slice(c0, c1)), ALU.mult, ALU.mult)
                    for b0 in range(c0, c1, 128):
                        n = min(128, c1 - b0)
                        sl = cnt["fo"] % 8
                        cnt["fo"] += 1
                        for half in range(2):
                            bk = tbank()
                            for kk in range(4):
                                k = half * 4 + kk
                                tr(bk(slice(kk * 128, kk * 128 + 128), p=(0, n)), xT(k, slice(b0, b0 + n)), ident())
                            copy(lstg(sl, slice(half * 512, half * 512 + 512), p=(0, n)), bk(slice(0, 512), p=(0, n)))
                        if typ == 'p':
                            row = g.lo + b0 - 256
                            dma("sp", y_main[row:row + n, :], lstg.ap[0:n, sl, :], [lstg(sl)], [], f"ld{sl}", is_out=True)
                        else:
                            dma("sp", y_samp[:, :], lstg.ap[0:n, sl, :], [lstg(sl)], [], f"ld{sl}", is_out=True)

            def dbg_dump(tag):
                if dbg_point == tag:
                    dma("sp", dbgd[:, :], xT_t[:].rearrange("p a b -> p (a b)"), [xT()], [], "dbg", is_out=True)
                    raise _Stop()

            h1p_t = sb_h1p[0]
            h1p = LT("h1p", h1p_t[:], 0, 4, (8, 15))

            def sample_cache_prep():
                dma("pool", ovl_t[:, 45088 // 2: 45088 // 2 + 2048].rearrange("p (b c) -> p b c", b=16),
                    ckd.rearrange("b t c -> t b c"), [], [kctok.whole()], "kc")
                dma("pool", ovl_t[:, 49184 // 2: 49184 // 2 + 2048].rearrange("p (b c) -> p b c", b=16),
                    cvd.rearrange("b t c -> t b c"), [], [Vs.whole()], "vc")
                dma("sp", ksd[:, 0:124, :], ckd[:, 4:128, :], [], [], "cpk", is_out=True)
                dma("sp", vsd[:, 0:124, :], cvd[:, 4:128, :], [], [], "cpv", is_out=True)
                for r in range(2):
                    for bb in range(8):
                        b = r * 8 + bb
                        tr(tpb(slice(bb * 128, bb * 128 + 128)), kctok(b), identb())
                    src = LT("tpb", tpb_t[:].rearrange("p (a b) -> p a b", a=8), 0, 2, (8, 128))
                    copy(kTs(slice(r * 8, r * 8 + 8)), src())

            sample_cache_prep()
            for gi_, g in enumerate(GROUPS):
              try:
                load_x(g)
                dbg_dump(f"g{gi_}.load")
                if g.S:
                    conv_hist_load(g)
                ffn(g, 0, g.n, 0, 0)
                dbg_dump(f"g{gi_}.ffn0")
                mixer(g)
                if g.S:
                    pool_hist_dma(g)
                dbg_dump(f"g{gi_}.mix")
                ffn(g, g.noA, g.n, 1, 2)
                if g.S:
                    pool_hist_tr(g)
                dbg_dump(f"g{gi_}.ffn1")
                ffn(g, g.noA, g.n, 2, 3)
                dbg_dump(f"g{gi_}.ffn2")
                pool_stage(g)
                if g is not GROUPS[2]:
                    copy(h1p(), h1c(), eng="dve")
                dbg_dump(f"g{gi_}.pool")
                ffn(g, g.out, g.n, 3, 5)
                dbg_dump(f"g{gi_}.ffn3")
                final_out(g)
              except _Stop:
                break

        sb_h1p = [sb("h1p", [128, 8, 15], F32)]

        wspecs = []
        emit(Sched(True), wspecs)
        S = Sched(False)
        emit(S, wspecs)
        S.finalize()

        ops = S.ops
        engs = ["pe", "act", "dve", "pool", "sp"]
        sigval = {}
        c = {e: 0 for e in engs}
        for i, o in enumerate(ops):
            if o["signal"]:
                c[o["eng"]] += 1
                sigval[i] = c[o["eng"]]
        dma_keys = sorted({o["dma"] for o in ops if o["dma"] is not None})
        sems = {}
        for e in engs:
            sems[("eng", e)] = es.enter_context(nc.semaphore(f"s_{e}"))
        for k in dma_keys:
            sems[("dma", k)] = es.enter_context(nc.semaphore(f"d_{k}"))
        block = es.enter_context(nc.Block())

        def run_engine(ename, e):
            waited = {}
            for i, o in enumerate(ops):
                if o["eng"] != ename:
                    continue
                for k, v in o["waits"].items():
                    val = v if k[0] == "dma" else sigval[v]
                    if waited.get(k, 0) >= val:
                        continue
                    e.wait_ge(sems[k], val)
                    waited[k] = val
                if o["fn"] is None:
                    continue
                inst = o["fn"](e)
                if o["dma"] is not None:
                    inst.then_inc(sems[("dma", o["dma"])], 16)
                elif o["signal"]:
                    inst.then_inc(sems[("eng", ename)], 1)

        @block.tensor
        def _(e):
            run_engine("pe", e)

        @block.scalar
        def _(e):
            run_engine("act", e)

        @block.vector
        def _(e):
            run_engine("dve", e)

        @block.gpsimd
        def _(e):
            run_engine("pool", e)

        @block.sync
        def _(e):
            run_engine("sp", e)

    return nc


_CACHE = {}


def _prep_weights(ffn_w_gate, ffn_w_up, ffn_w_down, mix_w_in, mix_w_out, pool_w):
    f32 = np.float32
    wgu = np.empty((4, NJ, 128, 2, KC, 128), f32)
    wd = np.empty((4, 8, 2, 128, 11, 128), f32)
    for l in range(2):
        for i in range(2):
            f = l * 2 + i
            g = np.asarray(ffn_w_gate[l, i]).reshape(KC, 128, NJ, 128).transpose(2, 1, 0, 3)
            u = np.asarray(ffn_w_up[l, i]).reshape(KC, 128, NJ, 128).transpose(2, 1, 0, 3)
            wgu[f, :, :, 0] = g
            wgu[f, :, :, 1] = u
            dn = np.asarray(ffn_w_down[l, i]).reshape(2, 11, 128, 8, 128).transpose(3, 0, 2, 1, 4)
            wd[f] = dn
    wgu = wgu.reshape(4, NJ, 128, 2048)
    wd = wd.reshape(4, 8, 2, 128, 1408)
    W = np.asarray(mix_w_in[0])
    tiles = []
    r = np.arange

    def head(h):
        return r(h * 64, h * 64 + 64)

    def headp(h):
        return np.concatenate([r(h * 64 + 32, h * 64 + 64), r(h * 64, h * 64 + 32)])

    for hp in range(4):
        tiles.append(np.concatenate([head(hp), head(hp + 4)]))
    for hp in range(4):
        tiles.append(np.concatenate([headp(hp), headp(hp + 4)]))
    tiles.append(512 + r(128))
    tiles.append(512 + np.concatenate([headp(0), headp(1)]))
    tiles.append(640 + r(128))
    for i in range(4):
        tiles.append(1280 + i * 128 + r(128))
    for i in range(4):
        tiles.append(1792 + i * 128 + r(128))
    for i in range(4):
        tiles.append(768 + i * 128 + r(128))
    wmi = np.stack([W[:, t].reshape(KC, 128, 128).transpose(1, 0, 2).reshape(128, 1024) for t in tiles]).astype(f32)
    rows = []
    for hp in range(4):
        rows += [head(hp), head(hp + 4)]
    rows.append(512 + r(512))
    rows = np.concatenate(rows)
    Wo = np.asarray(mix_w_out[0])[rows, :]
    wmo = Wo.reshape(KC, 128, 8, 128).transpose(2, 1, 0, 3).reshape(8, 128, 1024).astype(f32)
    wpl = np.asarray(pool_w[0]).reshape(4, 2, 128, 2, 128).transpose(2, 0, 3, 1, 4).reshape(128, 2048).astype(f32)
    return dict(wgu=np.ascontiguousarray(wgu), wd=np.ascontiguousarray(wd), wmi=np.ascontiguousarray(wmi),
                wmo=np.ascontiguousarray(wmo), wpl=np.ascontiguousarray(wpl))


def _core_inputs(c, x_prompt, x_sample, cache_k, cache_v, state_conv, state_pool, meta_tokens, ln_gain,
                 attn_sink, conv_w, pool_scale, final_gain):
    f32 = np.float32
    b, half = c // 2, c % 2
    xin = np.zeros((TCORE, D), f32)
    pos = np.zeros(TCORE, np.int64)
    if half == 0:
        xin[240:256] = meta_tokens
        pos[128:256] = np.arange(128) - 112
        xin[256:2304] = x_prompt[b, 0:2048]
        pos[256:2304] = 16 + np.arange(2048)
    else:
        xin[0:256] = x_prompt[b, 1792:2048]
        pos[0:256] = 16 + 1792 + np.arange(256)
        xin[256:2304] = x_prompt[b, 2048:4096]
        pos[256:2304] = 16 + 2048 + np.arange(2048)
    xs = x_sample[16 * c:16 * c + 16]
    xin[2304:2368] = xs.transpose(1, 0, 2).reshape(64, D)
    pos[2304:2368] = 8192 + np.repeat(np.arange(4), 16)
    inv = (np.float32(10000.0) ** (-np.arange(32, dtype=f32) / np.float32(32))).astype(f32)
    ang = pos.astype(f32)[:, None] * inv[None, :]
    cosv = np.cos(ang).astype(f32).T
    sinv = np.sin(ang).astype(f32).T
    cosd = np.concatenate([cosv, cosv, cosv, cosv], 0)
    sind = np.concatenate([-sinv, sinv, -sinv, sinv], 0)
    cst = np.zeros((128, NCST), f32)
    gl = [ln_gain[0, 0], ln_gain[0, 1], ln_gain[0, 2], ln_gain[1, 0], ln_gain[1, 1], ln_gain[1, 2], final_gain]
    for gi, gv in enumerate(gl):
        cst[:, C_GAIN + gi * 8:C_GAIN + gi * 8 + 8] = np.asarray(gv).reshape(8, 128).T
    for j in range(3):
        cst[:, C_CONVW + j * 4:C_CONVW + j * 4 + 4] = np.asarray(conv_w[0, j]).reshape(4, 128).T
    cst[:, C_PSC:C_PSC + 8] = np.asarray(pool_scale[0]).reshape(8, 128).T
    cst[:, C_SINK:C_SINK + 8] = np.asarray(attn_sink[0]).reshape(1, 8)
    rr = np.arange(128)
    for kvh in range(2):
        cst[:, C_SINKS + kvh] = np.asarray(attn_sink[0, kvh])[(rr % 16) // 4]
    i_ = np.arange(128)[:, None]
    j_ = np.arange(256)[None, :]
    std = (j_ >= i_) & (j_ <= i_ + 128)
    mB = std.copy()
    mM0 = std.copy()
    if half == 0:
        mB &= (j_ >= 128 + 112)
        mM0 &= (j_ >= 112)
    for mi, m in enumerate((std, mB, mM0)):
        cst[:, C_MASK + mi * 256:C_MASK + mi * 256 + 256] = np.where(m, 0.0, NEG)
    tok = (rr % 4)[:, None]
    ii = (rr // 16)[:, None]
    jc = np.arange(128)[None, :]
    mc = jc >= tok
    cn = np.arange(32)[None, :]
    mn = ((cn % 8) == ii) & ((cn // 8) <= tok)
    cst[:, C_MASKS:C_MASKS + 160] = np.where(np.concatenate([mc, mn], 1), 0.0, NEG)
    cst[:, C_ID:C_ID + 128] = np.eye(128, dtype=f32)
    return dict(
        xin=xin, cosd=np.ascontiguousarray(cosd), sind=np.ascontiguousarray(sind), cst=cst,
        ck=np.ascontiguousarray(cache_k[0, 16 * c:16 * c + 16].reshape(16, 128, 128)),
        cv=np.ascontiguousarray(cache_v[0, 16 * c:16 * c + 16].reshape(16, 128, 128)),
        sconv=np.ascontiguousarray(state_conv[0, 16 * c:16 * c + 16]),
        spool=np.ascontiguousarray(state_pool[0, 16 * c:16 * c + 16]),
    )


def kernel(x_prompt, x_sample, cache_k, cache_v, state_conv, state_pool, meta_tokens, ln_gain, ffn_w_gate,
           ffn_w_up, ffn_w_down, mix_w_in, attn_sink, conv_w, mix_w_out, pool_w, pool_scale, final_gain,
           _dbg=None, _ncores=8):
    A = lambda v: np.asarray(v, dtype=np.float32)
    x_prompt, x_sample, cache_k, cache_v = A(x_prompt), A(x_sample), A(cache_k), A(cache_v)
    state_conv, state_pool, meta_tokens, ln_gain = A(state_conv), A(state_pool), A(meta_tokens), A(ln_gain)
    attn_sink, conv_w, pool_scale, final_gain = A(attn_sink), A(conv_w), A(pool_scale), A(final_gain)
    wts = _prep_weights(A(ffn_w_gate), A(ffn_w_up), A(ffn_w_down), A(mix_w_in), A(mix_w_out), A(pool_w))
    key = ("nc", _dbg)
    if key not in _CACHE:
        _CACHE[key] = build_program(_dbg)
    nc = _CACHE[key]
    in_maps = []
    for c in range(_ncores):
        m = _core_inputs(c, x_prompt, x_sample, cache_k, cache_v, state_conv, state_pool, meta_tokens, ln_gain,
                         attn_sink, conv_w, pool_scale, final_gain)
        m.update(wts)
        in_maps.append(m)
    res = run_bass_kernel_spmd(nc, in_maps, core_ids=list(range(_ncores)))
    R = res.results
    if _dbg is not None:
        return R
    f32 = np.float32
    y_prompt = np.empty((4, 4096, D), f32)
    y_sample = np.empty((128, 4, D), f32)
    k_p = np.empty((1, 4, 128, 2, 64), f32)
    v_p = np.empty((1, 4, 128, 2, 64), f32)
    conv_p = np.empty((1, 4, 2, 512), f32)
    pool_p = np.empty((1, 4, 15, D), f32)
    k_s = np.empty((1, 128, 128, 2, 64), f32)
    v_s = np.empty((1, 128, 128, 2, 64), f32)
    conv_s = np.empty((1, 128, 2, 512), f32)
    pool_s = np.empty((1, 128, 15, D), f32)
    for c in range(8):
        b, half = c // 2, c % 2
        r = R[c]
        y_prompt[b, half * 2048:(half + 1) * 2048] = r["y_main"]
        y_sample[16 * c:16 * c + 16] = r["y_samp"].reshape(4, 16, D).transpose(1, 0, 2)
        if half == 1:
            k_p[0, b] = r["kp"].reshape(128, 2, 64)
            v_p[0, b] = r["vp"].reshape(128, 2, 64)
            conv_p[0, b] = r["convp"]
            pool_p[0, b] = r["poolp"]
        k_s[0, 16 * c:16 * c + 16] = r["ks"].reshape(16, 128, 2, 64)
        v_s[0, 16 * c:16 * c + 16] = r["vs"].reshape(16, 128, 2, 64)
        conv_s[0, 16 * c:16 * c + 16] = r["convs"]
        pool_s[0, 16 * c:16 * c + 16] = r["pools"]
    return (y_prompt, y_sample, k_p, v_p, conv_p, pool_p, k_s, v_s, conv_s, pool_s)
```
